# Optimizing a Trainium2 kernel written in Bass

```python
import math
import numpy as np
import jax
import jax.numpy as jnp
from jax import lax


D_MODEL = 1024
BATCH = 8
SEQ = 2048
DEPTH = 4

D_FF = 2752
HEAD_DIM = 64
HEAD_SLOTS = 16
NUM_BUCKETS = 32
T5_MAX_EXACT = 16
T5_MAX_DIST = 128
Q_BLOCK = 128
GATHER_Q_BLOCK = 32
RMS_EPS = 1e-6
NEG_INF = -1e30
MLA_HEADS = 8
MLA_NOPE = 64
MLA_ROPE = 32
MLA_V = 64
MLA_Q_LORA = 256
MLA_KV_LORA = 128
ROPE_THETA = 10000.0
NSA_HEADS = 8
NSA_GROUPS = 2
NSA_HPG = NSA_HEADS // NSA_GROUPS
NSA_CMP_LEN = 32
NSA_CMP_STRIDE = 16
NSA_CMP_HID = 256
NSA_SLC_BLOCK = 64
NSA_SLC_TOP = 8
NSA_WINDOW = 512
NSA_FORCED = 1e6
DIL_PAIRS = ((128, 1), (512, 4), (2048, 16))
DIL_HPG = 4
DIL_SLOTS = len(DIL_PAIRS) * DIL_HPG
MOBA_HEADS = 4
MOBA_BLOCK = 256
MOBA_TOP = 3
EV_SPLITS = (MLA_Q_LORA, MLA_KV_LORA, MLA_ROPE, NSA_HEADS * HEAD_DIM) + (NSA_GROUPS * HEAD_DIM,) * 6 + (3 * NSA_HEADS,)
EV_COLS = sum(EV_SPLITS)
EV_OUT = MLA_HEADS * MLA_V + NSA_HEADS * HEAD_DIM
OD_SPLITS = (DIL_SLOTS * HEAD_DIM,) * 3 + (MOBA_HEADS * HEAD_DIM,) * 3
OD_COLS = sum(OD_SPLITS)
OD_OUT = DIL_HPG * HEAD_DIM + MOBA_HEADS * HEAD_DIM
N_EVEN = (DEPTH + 1) // 2
N_ODD = DEPTH // 2

kernel_name = 'hybrid_mla_nsa_dilated_moba_macaron'


def split_cols(u, sizes):
    return jnp.split(u, np.cumsum(sizes)[:-1].tolist(), axis=-1)


def rms_norm(x, g):
    x32 = x.astype(jnp.float32)
    y = x32 * lax.rsqrt(jnp.mean(x32 * x32, axis=-1, keepdims=True) + RMS_EPS)
    return (y * g.astype(jnp.float32)).astype(x.dtype)


def masked_softmax(logits, mask):
    l = jnp.where(mask, logits, NEG_INF)
    m = jnp.max(l, axis=-1, keepdims=True)
    e = jnp.where(mask, jnp.exp(l - m), 0.0)
    den = jnp.maximum(jnp.sum(e, axis=-1, keepdims=True), 1e-30)
    return e / den, (m + jnp.log(den))[..., 0]


def t5_bucket(dist):
    n = jnp.maximum(jnp.asarray(dist, jnp.int32), 0)
    nf = jnp.maximum(n, 1).astype(jnp.float32)
    large = T5_MAX_EXACT + (jnp.log(nf / T5_MAX_EXACT) / math.log(T5_MAX_DIST / T5_MAX_EXACT)
                            * (NUM_BUCKETS - T5_MAX_EXACT)).astype(jnp.int32)
    return jnp.where(n < T5_MAX_EXACT, n, jnp.minimum(large, NUM_BUCKETS - 1))


def from_blocks(a):
    n, b, qb = a.shape[:3]
    return jnp.moveaxis(a, 0, 1).reshape(b, n * qb, *a.shape[3:])


def rope_tables(s):
    inv = ROPE_THETA ** (-jnp.arange(0, MLA_ROPE, 2, dtype=jnp.float32) / MLA_ROPE)
    ang = jnp.arange(s, dtype=jnp.float32)[:, None] * inv[None, :]
    return jnp.cos(ang), jnp.sin(ang)


def apply_rope(x, cos, sin):
    x32 = x.astype(jnp.float32)
    half = x.shape[-1] // 2
    x1, x2 = x32[..., :half], x32[..., half:]
    return jnp.concatenate([x1 * cos - x2 * sin, x2 * cos + x1 * sin], axis=-1).astype(x.dtype)


def swiglu(y, w_in, w_out):
    a, b = jnp.split(y @ w_in, 2, axis=-1)
    return (jax.nn.silu(a) * b) @ w_out


def modulate(x, g, shift, scale):
    return rms_norm(x, g) * (1.0 + scale[:, None, :]) + shift[:, None, :]


def causal_dense(q, k, v):
    b, s = q.shape[:2]
    scale = q.shape[-1] ** -0.5
    kpos = jnp.arange(s)

    def block(blk):
        qs = blk * Q_BLOCK
        qb = lax.dynamic_slice_in_dim(q, qs, Q_BLOCK, axis=1)
        qpos = qs + jnp.arange(Q_BLOCK)
        logits = jnp.einsum('bqhd,bkhd->bhqk', qb, k).astype(jnp.float32) * scale
        p, _ = masked_softmax(logits, kpos[None, :] <= qpos[:, None])
        return jnp.einsum('bhqk,bkhd->bqhd', p.astype(v.dtype), v)

    return from_blocks(lax.map(block, jnp.arange(s // Q_BLOCK)))


def band_attn(q, k, v, bias_tab, w, r):
    b, s, g, p_, dh = q.shape
    scale = dh ** -0.5
    nk = w // r + 1
    dist = r * np.arange(nk)
    bias = jnp.transpose(bias_tab[t5_bucket(dist)], (1, 2, 0))[:, :, None, :]
    kp = jnp.pad(k, ((0, 0), (w, 0), (0, 0), (0, 0)))
    vp = jnp.pad(v, ((0, 0), (w, 0), (0, 0), (0, 0)))
    gidx = np.arange(Q_BLOCK)[:, None] + w - dist[None, :]
    dist_j = jnp.asarray(dist)

    def block(blk):
        qs = blk * Q_BLOCK
        qb = lax.dynamic_slice_in_dim(q, qs, Q_BLOCK, axis=1)
        kg = lax.dynamic_slice_in_dim(kp, qs, w + Q_BLOCK, axis=1)[:, gidx]
        vg = lax.dynamic_slice_in_dim(vp, qs, w + Q_BLOCK, axis=1)[:, gidx]
        ok = (qs + jnp.arange(Q_BLOCK)[:, None] - dist_j[None, :]) >= 0
        logits = jnp.einsum('bqgpd,bqkgd->bgpqk', qb, kg).astype(jnp.float32) * scale + bias
        pr, lse = masked_softmax(logits, ok)
        out = jnp.einsum('bgpqk,bqkgd->bqgpd', pr.astype(vg.dtype), vg)
        return out, lse

    outs, lses = lax.map(block, jnp.arange(s // Q_BLOCK))
    lse = jnp.transpose(lses, (1, 0, 4, 2, 3)).reshape(b, s, g, p_)
    return from_blocks(outs), lse


def mla_mixer(cq, ckv, kr, q_norm_g, kv_norm_g, w_uq, w_ukv, qk_g):
    b, s = cq.shape[:2]
    q = (rms_norm(cq, q_norm_g) @ w_uq).reshape(b, s, MLA_HEADS, MLA_NOPE + MLA_ROPE)
    kv = (rms_norm(ckv, kv_norm_g) @ w_ukv).reshape(b, s, MLA_HEADS, MLA_NOPE + MLA_V)
    cos, sin = rope_tables(s)
    q_nope = rms_norm(q[..., :MLA_NOPE], qk_g[0, :MLA_NOPE])
    q_rope = apply_rope(rms_norm(q[..., MLA_NOPE:], qk_g[0, MLA_NOPE:]), cos[:, None, :], sin[:, None, :])
    k_nope = rms_norm(kv[..., :MLA_NOPE], qk_g[1, :MLA_NOPE])
    k_rope = apply_rope(rms_norm(kr, qk_g[1, MLA_NOPE:]), cos, sin)
    qf = jnp.concatenate([q_nope, q_rope], axis=-1)
    kf = jnp.concatenate([k_nope, jnp.broadcast_to(k_rope[:, :, None, :], (b, s, MLA_HEADS, MLA_ROPE))], axis=-1)
    out = causal_dense(qf, kf, kv[..., MLA_NOPE:])
    return out.reshape(b, s, MLA_HEADS * MLA_V)


def nsa_mixer(q, kc, vc, ks, vs, kw, vw, gate_logits, cmp_pe, cmp_w1, cmp_w2, qk_g, bias_tab):
    b, s = q.shape[:2]
    g, p_, dh = NSA_GROUPS, NSA_HPG, HEAD_DIM
    scale = dh ** -0.5
    q = rms_norm(q.reshape(b, s, g, p_, dh), qk_g[0])
    kc, vc, ks, vs, kw, vw = [a.reshape(b, s, g, dh) for a in (kc, vc, ks, vs, kw, vw)]
    ks = rms_norm(ks, qk_g[1])
    kw = rms_norm(kw, qk_g[1])
    t = jnp.arange(s)

    n_cmp = (s - NSA_CMP_LEN) // NSA_CMP_STRIDE + 1
    cstart = np.arange(n_cmp) * NSA_CMP_STRIDE
    cidx = cstart[:, None] + np.arange(NSA_CMP_LEN)[None, :]

    def compress(a, j):
        blk = a[:, cidx] + cmp_pe[j][None, None, :, None, :]
        flat = jnp.moveaxis(blk, 3, 2).reshape(b, n_cmp, g, NSA_CMP_LEN * dh)
        return jax.nn.silu(flat @ cmp_w1[j]) @ cmp_w2[j]

    k_cmp = rms_norm(compress(kc, 0), qk_g[1])
    v_cmp = compress(vc, 1)
    cend = jnp.asarray(cstart + NSA_CMP_LEN - 1)
    mask_c = cend[None, :] <= t[:, None]
    bias_c = jnp.transpose(bias_tab[t5_bucket(t[:, None] - cend[None, :])], (2, 3, 0, 1))
    logits_c = jnp.einsum('bsgpd,bngd->bgpsn', q, k_cmp).astype(jnp.float32) * scale + bias_c
    p_c, _ = masked_softmax(logits_c, mask_c)
    o_c = jnp.einsum('bgpsn,bngd->bsgpd', p_c.astype(v_cmp.dtype), v_cmp)

    n_slc = s // NSA_SLC_BLOCK
    sstart = np.arange(n_slc) * NSA_SLC_BLOCK
    overlap = np.clip(np.minimum(cstart[:, None] + NSA_CMP_LEN, sstart[None, :] + NSA_SLC_BLOCK)
                      - np.maximum(cstart[:, None], sstart[None, :]), 0, None).astype(np.float32) / NSA_CMP_LEN
    imp = jnp.einsum('bgpsn,nm->bgsm', p_c, jnp.asarray(overlap))
    cur = t // NSA_SLC_BLOCK
    ids = jnp.arange(n_slc)[None, :]
    forced = (ids == 0) | (ids == cur[:, None]) | (ids == cur[:, None] - 1)
    score = jnp.where(forced, NSA_FORCED, jnp.where(ids <= cur[:, None], imp, NEG_INF))
    n_sel = min(NSA_SLC_TOP, n_slc)
    top_s, top_i = lax.top_k(score, n_sel)
    sel_ok = top_s > 0.5 * NEG_INF
    k_blk = jnp.transpose(ks.reshape(b, n_slc, NSA_SLC_BLOCK, g, dh), (0, 3, 1, 2, 4))
    v_blk = jnp.transpose(vs.reshape(b, n_slc, NSA_SLC_BLOCK, g, dh), (0, 3, 1, 2, 4))
    bi = jnp.arange(b)[:, None, None, None]
    gi = jnp.arange(g)[None, :, None, None]
    qbn = GATHER_Q_BLOCK

    def sel_block(blk):
        qs = blk * qbn
        qb = lax.dynamic_slice_in_dim(q, qs, qbn, axis=1)
        ib = lax.dynamic_slice_in_dim(top_i, qs, qbn, axis=2)
        okb = lax.dynamic_slice_in_dim(sel_ok, qs, qbn, axis=2)
        kg = k_blk[bi, gi, ib].reshape(b, g, qbn, n_sel * NSA_SLC_BLOCK, dh)
        vg = v_blk[bi, gi, ib].reshape(b, g, qbn, n_sel * NSA_SLC_BLOCK, dh)
        pos = (ib[..., None] * NSA_SLC_BLOCK + jnp.arange(NSA_SLC_BLOCK)).reshape(b, g, qbn, -1)
        dist = (qs + jnp.arange(qbn))[None, None, :, None] - pos
        mask = (jnp.repeat(okb, NSA_SLC_BLOCK, axis=-1) & (dist >= 0))[:, :, None]
        bias = jnp.moveaxis(bias_tab[t5_bucket(dist), gi], -1, 2)
        logits = jnp.einsum('bqgpd,bgqkd->bgpqk', qb, kg).astype(jnp.float32) * scale + bias
        pr, _ = masked_softmax(logits, mask)
        return jnp.einsum('bgpqk,bgqkd->bqgpd', pr.astype(vg.dtype), vg)

    o_s = from_blocks(lax.map(sel_block, jnp.arange(s // qbn)))

    o_w, _ = band_attn(q, kw, vw, bias_tab, NSA_WINDOW - 1, 1)

    gt = jax.nn.sigmoid(gate_logits.astype(jnp.float32)).reshape(b, s, g, p_, 3).astype(q.dtype)
    out = gt[..., 0:1] * o_c + gt[..., 1:2] * o_s + gt[..., 2:3] * o_w
    return out.reshape(b, s, NSA_HEADS * dh)


def dilated_mixer(q, k, v, qk_g, t5_bias):
    b, s = q.shape[:2]
    ng = len(DIL_PAIRS)
    q = rms_norm(q.reshape(b, s, ng, DIL_HPG, HEAD_DIM), qk_g[0])
    k = rms_norm(k.reshape(b, s, ng, DIL_HPG, HEAD_DIM), qk_g[1])
    v = v.reshape(b, s, ng, DIL_HPG, HEAD_DIM)
    outs, lses = [], []
    for gidx, (w, r) in enumerate(DIL_PAIRS):
        tab = t5_bias[:, gidx * DIL_HPG:(gidx + 1) * DIL_HPG][:, :, None]
        o, lse = band_attn(q[:, :, gidx, :, None], k[:, :, gidx], v[:, :, gidx], tab, w, r)
        outs.append(o[:, :, :, 0])
        lses.append(lse[..., 0])
    wts = jax.nn.softmax(jnp.stack(lses), axis=0)
    out = jnp.sum(wts[..., None].astype(q.dtype) * jnp.stack(outs), axis=0)
    return out.reshape(b, s, DIL_HPG * HEAD_DIM)


def moba_mixer(q, k, v, qk_g, bias_tab):
    b, s = q.shape[:2]
    h, dh = MOBA_HEADS, HEAD_DIM
    scale = dh ** -0.5
    q = rms_norm(q.reshape(b, s, h, dh), qk_g[0])
    k = rms_norm(k.reshape(b, s, h, dh), qk_g[1])
    v = v.reshape(b, s, h, dh)
    nb = -(-s // MOBA_BLOCK)
    pad = nb * MOBA_BLOCK - s
    kp = jnp.pad(k, ((0, 0), (0, pad), (0, 0), (0, 0)))
    vp = jnp.pad(v, ((0, 0), (0, pad), (0, 0), (0, 0)))
    k_blk = jnp.transpose(kp.reshape(b, nb, MOBA_BLOCK, h, dh), (0, 3, 1, 2, 4))
    v_blk = jnp.transpose(vp.reshape(b, nb, MOBA_BLOCK, h, dh), (0, 3, 1, 2, 4))
    t = jnp.arange(s)
    own = t // MOBA_BLOCK
    n_top = min(MOBA_TOP, nb - 1)
    if n_top > 0:
        k_mean = jnp.mean(k_blk.astype(jnp.float32), axis=3)
        gate = jnp.einsum('bshd,bhnd->bhsn', q.astype(jnp.float32), k_mean)
        past = jnp.arange(nb)[None, :] < own[:, None]
        top_s, top_i = lax.top_k(jnp.where(past, gate, NEG_INF), n_top)
        sel_ok = top_s > 0.5 * NEG_INF
    bi = jnp.arange(b)[:, None, None, None]
    hi = jnp.arange(h)[None, :, None, None]
    qbn = GATHER_Q_BLOCK

    def block(blk):
        qs = blk * qbn
        qb = lax.dynamic_slice_in_dim(q, qs, qbn, axis=1)
        qpos = qs + jnp.arange(qbn)
        ob = qs // MOBA_BLOCK
        ko = lax.dynamic_slice_in_dim(kp, ob * MOBA_BLOCK, MOBA_BLOCK, axis=1)
        vo = lax.dynamic_slice_in_dim(vp, ob * MOBA_BLOCK, MOBA_BLOCK, axis=1)
        d_own = qpos[:, None] - (ob * MOBA_BLOCK + jnp.arange(MOBA_BLOCK))[None, :]
        logits_own = (jnp.einsum('bqhd,bkhd->bhqk', qb, ko).astype(jnp.float32) * scale
                      + jnp.transpose(bias_tab[t5_bucket(d_own)], (2, 0, 1)))
        mask_own = jnp.broadcast_to(d_own >= 0, logits_own.shape)
        if n_top == 0:
            pr, _ = masked_softmax(logits_own, mask_own)
            return jnp.einsum('bhqk,bkhd->bqhd', pr.astype(vo.dtype), vo)
        ib = lax.dynamic_slice_in_dim(top_i, qs, qbn, axis=2)
        okb = lax.dynamic_slice_in_dim(sel_ok, qs, qbn, axis=2)
        nsel = n_top * MOBA_BLOCK
        kg = k_blk[bi, hi, ib].reshape(b, h, qbn, nsel, dh)
        vg = v_blk[bi, hi, ib].reshape(b, h, qbn, nsel, dh)
        pos = (ib[..., None] * MOBA_BLOCK + jnp.arange(MOBA_BLOCK)).reshape(b, h, qbn, nsel)
        logits_sel = (jnp.einsum('bqhd,bhqkd->bhqk', qb, kg).astype(jnp.float32) * scale
                      + bias_tab[t5_bucket(qpos[None, None, :, None] - pos), hi])
        mask_sel = jnp.repeat(okb, MOBA_BLOCK, axis=-1)
        pr, _ = masked_softmax(jnp.concatenate([logits_sel, logits_own], axis=-1),
                               jnp.concatenate([mask_sel, mask_own], axis=-1))
        pr = pr.astype(v.dtype)
        return (jnp.einsum('bhqk,bhqkd->bqhd', pr[..., :nsel], vg)
                + jnp.einsum('bhqk,bkhd->bqhd', pr[..., nsel:], vo))

    return from_blocks(lax.map(block, jnp.arange(s // qbn))).reshape(b, s, h * dh)


def even_mixer(y, w_in, w_out, q_norm_g, kv_norm_g, w_uq, w_ukv, mla_qk_g, cmp_pe, cmp_w1, cmp_w2, nsa_qk_g, t5_bias):
    cq, ckv, kr, qn, kc, vc, ks, vs, kw, vw, gl = split_cols(y @ w_in, EV_SPLITS)
    o_a = mla_mixer(cq, ckv, kr, q_norm_g, kv_norm_g, w_uq, w_ukv, mla_qk_g)
    tab = t5_bias[:, MLA_HEADS:MLA_HEADS + NSA_HEADS].reshape(NUM_BUCKETS, NSA_GROUPS, NSA_HPG)
    o_b = nsa_mixer(qn, kc, vc, ks, vs, kw, vw, gl, cmp_pe, cmp_w1, cmp_w2, nsa_qk_g, tab)
    return jnp.concatenate([o_a, o_b], axis=-1) @ w_out


def odd_mixer(y, w_in, w_out, dil_qk_g, moba_qk_g, t5_bias):
    qd, kd, vd, qm, km, vm = split_cols(y @ w_in, OD_SPLITS)
    o_c = dilated_mixer(qd, kd, vd, dil_qk_g, t5_bias)
    o_d = moba_mixer(qm, km, vm, moba_qk_g, t5_bias[:, DIL_SLOTS:DIL_SLOTS + MOBA_HEADS])
    return jnp.concatenate([o_c, o_d], axis=-1) @ w_out


def setup_inputs(seed: int = 0) -> dict:
    key = jax.random.key(seed)
    keys = iter(jax.random.split(key, 32))

    def nrm(shape, scale):
        return jax.random.normal(next(keys), shape, jnp.float32) * scale

    return {
        'x': nrm((BATCH, SEQ, D_MODEL), 1.0),
        'c': nrm((BATCH, D_MODEL), 1.0),
        't5_bias': nrm((NUM_BUCKETS, HEAD_SLOTS), 0.5),
        'ada_w': nrm((DEPTH, D_MODEL, 9 * D_MODEL), D_MODEL ** -0.5),
        'ada_b': nrm((DEPTH, 9 * D_MODEL), 0.02),
        'norm_g': 1.0 + nrm((DEPTH, 3, D_MODEL), 0.05),
        'ffn_w_in': nrm((DEPTH, 2, D_MODEL, 2 * D_FF), D_MODEL ** -0.5),
        'ffn_w_out': nrm((DEPTH, 2, D_FF, D_MODEL), D_FF ** -0.5),
        'ev_w_in': nrm((N_EVEN, D_MODEL, EV_COLS), D_MODEL ** -0.5),
        'ev_w_out': nrm((N_EVEN, EV_OUT, D_MODEL), EV_OUT ** -0.5),
        'mla_q_norm_g': 1.0 + nrm((N_EVEN, MLA_Q_LORA), 0.05),
        'mla_kv_norm_g': 1.0 + nrm((N_EVEN, MLA_KV_LORA), 0.05),
        'mla_w_uq': nrm((N_EVEN, MLA_Q_LORA, MLA_HEADS * (MLA_NOPE + MLA_ROPE)), MLA_Q_LORA ** -0.5),
        'mla_w_ukv': nrm((N_EVEN, MLA_KV_LORA, MLA_HEADS * (MLA_NOPE + MLA_V)), MLA_KV_LORA ** -0.5),
        'mla_qk_g': 1.0 + nrm((N_EVEN, 2, MLA_NOPE + MLA_ROPE), 0.05),
        'nsa_cmp_pe': nrm((N_EVEN, 2, NSA_CMP_LEN, HEAD_DIM), 0.1),
        'nsa_cmp_w1': nrm((N_EVEN, 2, NSA_CMP_LEN * HEAD_DIM, NSA_CMP_HID), (NSA_CMP_LEN * HEAD_DIM) ** -0.5),
        'nsa_cmp_w2': nrm((N_EVEN, 2, NSA_CMP_HID, HEAD_DIM), NSA_CMP_HID ** -0.5),
        'nsa_qk_g': 1.0 + nrm((N_EVEN, 2, HEAD_DIM), 0.05),
        'od_w_in': nrm((N_ODD, D_MODEL, OD_COLS), D_MODEL ** -0.5),
        'od_w_out': nrm((N_ODD, OD_OUT, D_MODEL), OD_OUT ** -0.5),
        'dil_qk_g': 1.0 + nrm((N_ODD, 2, HEAD_DIM), 0.05),
        'moba_qk_g': 1.0 + nrm((N_ODD, 2, HEAD_DIM), 0.05),
    }


def reference(x, c, t5_bias, ada_w, ada_b, norm_g, ffn_w_in, ffn_w_out, ev_w_in, ev_w_out,
              mla_q_norm_g, mla_kv_norm_g, mla_w_uq, mla_w_ukv, mla_qk_g, nsa_cmp_pe, nsa_cmp_w1,
              nsa_cmp_w2, nsa_qk_g, od_w_in, od_w_out, dil_qk_g, moba_qk_g):
    b, d = x.shape[0], x.shape[-1]
    c_act = jax.nn.silu(c)
    for i in range(DEPTH):
        mod = (c_act @ ada_w[i] + ada_b[i]).reshape(b, 3, 3, d)
        y = modulate(x, norm_g[i, 0], mod[:, 0, 0], mod[:, 0, 1])
        x = x + 0.5 * mod[:, 0, 2][:, None, :] * swiglu(y, ffn_w_in[i, 0], ffn_w_out[i, 0])
        y = modulate(x, norm_g[i, 1], mod[:, 1, 0], mod[:, 1, 1])
        j = i // 2
        if i % 2 == 0:
            m = even_mixer(y, ev_w_in[j], ev_w_out[j], mla_q_norm_g[j], mla_kv_norm_g[j], mla_w_uq[j],
                           mla_w_ukv[j], mla_qk_g[j], nsa_cmp_pe[j], nsa_cmp_w1[j], nsa_cmp_w2[j],
                           nsa_qk_g[j], t5_bias)
        else:
            m = odd_mixer(y, od_w_in[j], od_w_out[j], dil_qk_g[j], moba_qk_g[j], t5_bias)
        x = x + mod[:, 1, 2][:, None, :] * m
        y = modulate(x, norm_g[i, 2], mod[:, 2, 0], mod[:, 2, 1])
        x = x + 0.5 * mod[:, 2, 2][:, None, :] * swiglu(y, ffn_w_in[i, 1], ffn_w_out[i, 1])
    return x
```

```python
import math
import contextlib
import numpy as np
import concourse.bass as bass
import concourse.mybir as mybir

F32 = mybir.dt.float32
BF16 = mybir.dt.bfloat16
AF = mybir.ActivationFunctionType
ALU = mybir.AluOpType
AX = mybir.AxisListType


class Prog:
    EPOCH = 12000
    NDMA = 12

    def __init__(self, nc):
        self.nc = nc
        self.es = contextlib.ExitStack()
        self.stacks = [self.es]
        self.eng = {"pe": nc.tensor, "act": nc.scalar, "dve": nc.vector, "pool": nc.gpsimd, "sp": nc.sync}
        self.cnt = {e: 0 for e in self.eng}
        self.sem = {}
        self.nsem = 0
        for e in ("pe", "act", "dve", "pool"):
            self.sem[e] = self._newsem(e)
        self.dsem = {q: [self._newsem(f"d{q}{i}") for i in range(self.NDMA)] for q in ("sp", "pool")}
        self.dcnt = {q: 0 for q in ("sp", "pool")}
        self.dval = {q: [0] * self.NDMA for q in ("sp", "pool")}
        self.last_w = {}
        self.readers = {}
        self.waited = {e: {} for e in self.eng}
        self.all_tokens = []

    def _newsem(self, name):
        self.nsem += 1
        return self.es.enter_context(self.nc.semaphore(f"s_{name}_{self.nsem}"))

    def sbuf(self, name, shape, dt):
        self.nalloc = getattr(self, "nalloc", 0) + 1
        return self.stacks[-1].enter_context(self.nc.sbuf_tensor(f"{name}_{self.nalloc}", list(shape), dt))

    @contextlib.contextmanager
    def phase(self):
        st = contextlib.ExitStack()
        self.stacks.append(st)
        try:
            yield
        finally:
            self.barrier()
            self.stacks.pop()
            st.close()

    def psum(self, name, shape, dt):
        return self.es.enter_context(self.nc.psum_tensor(name, list(shape), dt))

    def _wait(self, e, tok):
        sem, val, sid = tok
        w = self.waited[e]
        if w.get(sid, 0) >= val:
            return
        self.eng[e].wait_ge(sem, val)
        w[sid] = val

    def _deps(self, e, reads, writes, is_pe_mm):
        toks = []
        for k in reads:
            t = self.last_w.get(k)
            if t is not None:
                toks.append(t)
        for k in writes:
            t = self.last_w.get(k)
            if t is not None:
                toks.append(t)
            toks.extend(self.readers.get(k, ()))
        for t in toks:
            if is_pe_mm and t[3] == "pe":
                continue
            self._wait(e, t[:3])

    def _commit(self, tok, reads, writes):
        for k in reads:
            self.readers.setdefault(k, []).append(tok)
        for k in writes:
            self.last_w[k] = tok
            self.readers[k] = []

    def op(self, e, fn, reads=(), writes=()):
        pb = [r for r in reads if isinstance(r, str) and r.startswith("bank")]
        if pb:
            reads = [r for r in reads if r not in pb]
            writes = list(writes) + pb
        self._deps(e, reads, writes, e == "pe")
        ins = fn()
        if self.cnt[e] >= self.EPOCH:
            self.sem[e] = self._newsem(e)
            self.cnt[e] = 0
        self.cnt[e] += 1
        ins.then_inc(self.sem[e], 1)
        tok = (self.sem[e], self.cnt[e], id(self.sem[e]), e)
        self._commit(tok, reads, writes)
        return tok

    def dma(self, q, out, in_, reads=(), writes=(), **kw):
        j = self.dcnt[q]
        slot = j % self.NDMA
        sem = self.dsem[q][slot]
        if self.dval[q][slot] > 0:
            self._wait(q, (sem, self.dval[q][slot], id(sem)))
        self._deps(q, reads, writes, False)
        ins = self.eng[q].dma_start(out=out, in_=in_, **kw)
        self.dval[q][slot] += 16
        ins.then_inc(sem, 16)
        self.dcnt[q] += 1
        tok = (sem, self.dval[q][slot], id(sem), "dma_" + q)
        self._commit(tok, reads, writes)
        return tok

    def barrier(self):
        toks = []
        for e in ("pe", "act", "dve", "pool"):
            if self.cnt[e] > 0:
                toks.append((self.sem[e], self.cnt[e], id(self.sem[e])))
        for q in ("sp", "pool"):
            for i in range(self.NDMA):
                if self.dval[q][i] > 0:
                    toks.append((self.dsem[q][i], self.dval[q][i], id(self.dsem[q][i])))
        for e in self.eng:
            for t in toks:
                self._wait(e, t)
        self.last_w.clear()
        self.readers.clear()

    def finish(self):
        self.barrier()
        self.es.close()

    def mm(self, out, lhsT, rhs, start, stop=True, reads=(), writes=()):
        return self.op("pe", lambda: self.nc.tensor.matmul(out, lhsT=lhsT, rhs=rhs, start=start, stop=stop,
                                                            skip_group_check=True), reads, writes)

    def tr(self, out, in_, ident, reads=(), writes=()):
        return self.op("pe", lambda: self.nc.tensor.transpose(out, in_, ident), reads, writes)

    def act(self, out, in_, func, reads=(), writes=(), **kw):
        return self.op("act", lambda: self.nc.scalar.activation(out=out, in_=in_, func=func, **kw), reads, writes)

    def tt(self, e, out, in0, in1, op, reads=(), writes=()):
        return self.op(e, lambda: self.eng[e].tensor_tensor(out=out, in0=in0, in1=in1, op=op), reads, writes)

    def ts(self, e, out, in0, s1, s2, op0, op1=None, reads=(), writes=(), **kw):
        if op1 is None:
            return self.op(e, lambda: self.eng[e].tensor_scalar(out=out, in0=in0, scalar1=s1, scalar2=None, op0=op0, **kw),
                           reads, writes)
        return self.op(e, lambda: self.eng[e].tensor_scalar(out=out, in0=in0, scalar1=s1, scalar2=s2, op0=op0, op1=op1, **kw),
                       reads, writes)

    def stt(self, e, out, in0, scalar, in1, op0, op1, reads=(), writes=()):
        return self.op(e, lambda: self.eng[e].scalar_tensor_tensor(out=out, in0=in0, scalar=scalar, in1=in1, op0=op0, op1=op1),
                       reads, writes)

    def copy(self, e, out, in_, reads=(), writes=()):
        if e == "act":
            return self.op(e, lambda: self.nc.scalar.copy(out=out, in_=in_), reads, writes)
        return self.op(e, lambda: self.eng[e].tensor_copy(out=out, in_=in_), reads, writes)

    def memset(self, e, ap, val, writes=()):
        return self.op(e, lambda: self.eng[e].memset(ap, val), (), writes)

    def reduce(self, e, out, in_, op, axis=None, reads=(), writes=()):
        axis = AX.X if axis is None else axis
        return self.op(e, lambda: self.eng[e].tensor_reduce(out=out, in_=in_, axis=axis, op=op), reads, writes)
S = 2048
D = 1024
NT = 16
DFF = 2752
EPS = 1e-6
NEG = -30000.0
LEXT = 4096
OFF = 2047
LS = 768
LC = 4064

EV = dict(cq=0, ckv=256, kr=384, qn=416, kc=928, vc=1056, ks=1184, vs=1312, kw=1440, vw=1568, gl=1696, end=1720)


def t5_bucket_np(n):
    n = np.maximum(n, 0)
    nf = np.maximum(n, 1).astype(np.float32)
    large = 16 + (np.log(nf / np.float32(16)) / np.float32(math.log(128 / 16)) * np.float32(16)).astype(np.int32)
    return np.where(n < 16, n, np.minimum(large, 31))


def host_consts():
    i = np.arange(LEXT)
    dist = i - OFF
    bucket = t5_bucket_np(dist)
    allowed = [
        dist >= 0,
        (dist >= 0) & (dist <= 511),
        (dist >= 0) & (dist <= 128),
        (dist >= 0) & (dist <= 512) & (dist % 4 == 0),
        (dist >= 0) & (dist % 16 == 0),
    ]
    oh = np.zeros((5, 33, LEXT), np.float32)
    for v, al in enumerate(allowed):
        for b in range(32):
            oh[v, b] = ((bucket == b) & al).astype(np.float32)
        oh[v, 31] -= al.astype(np.float32)
        oh[v, 32] = np.where(al, 0.0, NEG)
    k = np.arange(S)
    exp32 = (k[None, :] // 64 == np.arange(32)[:, None]).astype(np.float32)
    exp8 = (k[None, :] // 256 == np.arange(8)[:, None]).astype(np.float32)
    cstart = np.arange(127) * 16
    sstart = np.arange(32) * 64
    overlap = np.clip(np.minimum(cstart[:, None] + 32, sstart[None, :] + 64)
                      - np.maximum(cstart[:, None], sstart[None, :]), 0, None).astype(np.float32) / 32
    inv = (np.float32(10000.0) ** (-np.arange(0, 32, 2, dtype=np.float32) / np.float32(32))).astype(np.float32)
    ang = (np.arange(S, dtype=np.float32)[:, None] * inv[None, :]).astype(np.float32)
    cos = np.cos(ang).astype(np.float32)
    sin = np.sin(ang).astype(np.float32)
    t = np.arange(S)
    cur = t // 64
    ids = np.arange(32)[None, :]
    forced = (ids == 0) | (ids == cur[:, None]) | (ids == cur[:, None] - 1)
    nsa_force = np.where(forced, 1e6, 0.0).astype(np.float32)
    nsa_valid = np.where(ids <= cur[:, None], 0.0, -1e30).astype(np.float32)
    return dict(k_oh=oh, k_exp32=exp32, k_exp8=exp8, k_overlap=overlap, k_cos=cos, k_sin=sin,
                k_force=nsa_force, k_valid=nsa_valid)


PARAM_SHAPES = dict(
    t5_bias=[32, 16], ada_w=[4, 1024, 9216], ada_b=[4, 9216], norm_g=[4, 3, 1024],
    ffn_w_in=[4, 2, 1024, 5504], ffn_w_out=[4, 2, 2752, 1024], ev_w_in=[2, 1024, 1720], ev_w_out=[2, 1024, 1024],
    mla_q_norm_g=[2, 256], mla_kv_norm_g=[2, 128], mla_w_uq=[2, 256, 768], mla_w_ukv=[2, 128, 1024],
    mla_qk_g=[2, 2, 96], nsa_cmp_pe=[2, 2, 32, 64], nsa_cmp_w1=[2, 2, 2048, 256], nsa_cmp_w2=[2, 2, 256, 64],
    nsa_qk_g=[2, 2, 64], od_w_in=[2, 1024, 3072], od_w_out=[2, 512, 1024], dil_qk_g=[2, 2, 64], moba_qk_g=[2, 2, 64],
)
CONST_SHAPES = dict(k_oh=[5, 33, LEXT], k_exp32=[32, S], k_exp8=[8, S], k_overlap=[127, 32], k_cos=[S, 16],
                    k_sin=[S, 16], k_force=[S, 32], k_valid=[S, 32])


class K:
    pass


def build(nsub=12, stages=None, start_sub=0):
    nc = bass.Bass("TRN2", target_bir_lowering=False)
    k = K()
    k.nc = nc
    class LazyIn(dict):
        def __missing__(self, n):
            shp = dict(x=[S, D], c=[D], **PARAM_SHAPES, **CONST_SHAPES)[n]
            self[n] = nc.dram_tensor(n, shp, F32, kind="ExternalInput").ap()
            return self[n]
    k.din = LazyIn()
    k.stages = stages
    k.out = nc.dram_tensor("out", [S, D], F32, kind="ExternalOutput").ap()
    k.xs = nc.dram_tensor("xs_scratch", [S, D], F32, kind="Internal").ap()
    P = Prog(nc)
    k.P = P
    k.yT = P.sbuf("yT", [128, 8, S], BF16)
    k.idb = P.sbuf("idb", [128, 128], BF16)
    k.idf = P.sbuf("idf", [128, 128], F32)
    k.mhalf = P.sbuf("mhalf", [128, 1], F32)
    k.cT = P.sbuf("cT", [128, 8], BF16)
    k.cTb = P.sbuf("cTb", [128, 8, 128], BF16)
    k.gcol = P.sbuf("gcol", [128, 12, 8], F32)
    k.A_col = P.sbuf("A_col", [128, 8], F32)
    k.B_col = P.sbuf("B_col", [128, 8], F32)
    k.gate_bc = P.sbuf("gate_bc", [128, D], F32)
    k.ssq = P.sbuf("ssq", [128, NT], F32)
    k.rstd = P.sbuf("rstd", [128, NT], F32)
    k.pb = [P.psum(f"bank{i}", [128, 512], F32) if i != 6 else None for i in range(8)]
    k.tpb = P.psum("bank6", [128, 1024], BF16)
    setup(k)
    subs = [(l, s) for l in range(4) for s in range(3)][start_sub:nsub]
    if any(s == 1 for _, s in subs):
        bias_setup(k)
    i = 0
    pending = None
    first = True
    done = False
    while not done:
        mixer_next = None
        with P.phase():
            k.X = P.sbuf("X", [128, NT, D], F32)
            src = k.din["x"] if first else k.xs
            first = False
            for t in range(NT):
                P.dma("sp", k.X[:, t, :], src[t * 128:(t + 1) * 128, :], reads=[("xs", t)], writes=[("X", t)])
            if pending is not None:
                out_proj(k, pending)
                pending = None
            while i < len(subs):
                layer, s3 = subs[i]
                i += 1
                adaln(k, layer, s3)
                norm_mod(k)
                if s3 != 1:
                    ffn(k, layer, 0 if s3 == 0 else 1)
                else:
                    for t in range(NT):
                        P.dma("sp", k.xs[t * 128:(t + 1) * 128, :], k.X[:, t, :], reads=[("X", t)], writes=[("xs", t)])
                    mixer_next = layer
                    break
            else:
                for t in range(NT):
                    P.dma("sp", k.out[t * 128:(t + 1) * 128, :], k.X[:, t, :], reads=[("X", t)])
                done = True
        if mixer_next is not None:
            with P.phase():
                if mixer_next % 2 == 0:
                    even_mixer(k, mixer_next // 2)
                else:
                    odd_mixer(k, mixer_next // 2)
            pending = mixer_next
    P.finish()
    k.used_inputs = list(k.din.keys())
    nc.used_inputs = k.used_inputs
    return nc


def out_proj(k, layer):
    P, nc = k.P, k.nc
    j = layer // 2
    if layer % 2 == 0:
        w_out, nch = k.din["ev_w_out"][j], 8
    else:
        w_out, nch = k.din["od_w_out"][j], 4
    with P.phase():
        wo = P.sbuf("wo_mix", [128, nch, D], BF16)
        tmp = [P.sbuf(f"otmp{i}", [128, 512], F32) for i in range(2)]
        P.dma("pool", wo[:], w_out.rearrange("(c p) d -> p c d", p=128), writes=["wo_mix"])
        n = 0
        for t in range(NT):
            for dh in range(2):
                ob = n % 2
                n += 1
                po = k.pb[4 + ob]
                for c in range(nch):
                    P.mm(po[:, :], k.yT[:, c, t * 128:(t + 1) * 128], wo[:, c, dh * 512:(dh + 1) * 512], c == 0, c == nch - 1,
                         reads=[("yT", t), "wo_mix"], writes=[f"bank{4 + ob}"])
                P.tt("dve", tmp[ob][:], po[:, :], k.gate_bc[:, dh * 512:(dh + 1) * 512], ALU.mult,
                     reads=[f"bank{4 + ob}", "gate_bc"], writes=[("otmp", ob)])
                xs_ = k.X[:, t, dh * 512:(dh + 1) * 512]
                P.tt("pool", xs_, xs_, tmp[ob][:], ALU.add, reads=[("otmp", ob), ("X", t)], writes=[("X", t)])


def build_AT(k, A, nch):
    P = k.P
    tp = k.tpb[:]
    for t in range(NT):
        for c in range(nch):
            P.tr(tp[:, c * 128:(c + 1) * 128], A[:, t, c * 128:(c + 1) * 128], k.idb[:], reads=[("A", t), "idb"],
                 writes=["bank6"])
        P.copy("dve", k.yT[:, 0:nch, t * 128:(t + 1) * 128], tp[:, 0:nch * 128].rearrange("p (c f) -> p c f", c=nch),
               reads=["bank6"], writes=[("yT", t)])


def setup(k):
    P, nc = k.P, k.nc
    P.memset("pool", k.idf[:], 0.0, writes=["idf"])
    P.op("pool", lambda: nc.gpsimd.affine_select(out=k.idf[:], in_=k.idf[:], pattern=[[-1, 128]],
                                                 compare_op=ALU.not_equal, fill=1.0, base=0, channel_multiplier=1),
         reads=["idf"], writes=["idf"])
    P.copy("dve", k.idb[:], k.idf[:], reads=["idf"], writes=["idb"])
    P.memset("pool", k.mhalf[:], -0.5, writes=["mhalf"])
    cf = P.sbuf("cf", [128, 8], F32)
    with nc.allow_non_contiguous_dma(reason="tiny column-layout loads"):
        P.dma("sp", cf[:], k.din["c"].rearrange("(k p) -> p k", p=128), writes=["cf"])
        for ls in range(12):
            P.dma("sp", k.gcol[:, ls, :], k.din["norm_g"][ls // 3, ls % 3].rearrange("(k p) -> p k", p=128),
                  writes=[("gcol", ls)])
    P.act(k.cT[:], cf[:], AF.Silu, reads=["cf"], writes=["cT"])
    P.copy("dve", k.cTb[:], k.cT[:].unsqueeze(2).to_broadcast([128, 8, 128]), reads=["cT"], writes=["cTb"])


def adaln(k, layer, s3):
    P, nc = k.P, k.nc
    with P.phase():
        wbuf = [P.sbuf(f"adaw{i}", [128, 8, 256], BF16) for i in range(2)]
        modbc = P.sbuf("modbc", [128, D], F32)
        bbc = P.sbuf("bbc", [128, D], F32)
        c0 = s3 * 3 * D
        bank = k.pb[7]
        ls = layer * 3 + s3
        jj = 0
        for v in range(3):
            P.dma("sp", bbc[:], k.din["ada_b"][layer, c0 + v * D:c0 + (v + 1) * D].partition_broadcast(128),
                  writes=["bbc"])
            for j in range(4):
                w = wbuf[jj % 2]
                cc = c0 + v * D + j * 256
                src = k.din["ada_w"][layer, :, cc:cc + 256].rearrange("(k p) c -> p k c", p=128)
                P.dma("pool", w[:], src, writes=[("adaw", jj % 2)])
                for kk in range(8):
                    P.mm(bank[:, 0:256], k.cTb[:, kk, :], w[:, kk, :], kk == 0, kk == 7,
                         reads=[("adaw", jj % 2), "cTb"], writes=["bank7"])
                dst = k.gate_bc if v == 2 else modbc
                P.tt("dve", dst[:, j * 256:(j + 1) * 256], bank[:, 0:256], bbc[:, j * 256:(j + 1) * 256], ALU.add,
                     reads=["bank7", "bbc"], writes=["modbc"])
                jj += 1
            if v == 2:
                if s3 != 1:
                    P.ts("dve", k.gate_bc[:], k.gate_bc[:], 0.5, None, ALU.mult, reads=["modbc"], writes=["gate_bc"])
                else:
                    P.ts("dve", k.gate_bc[:], k.gate_bc[:], 1.0, None, ALU.mult, reads=["modbc"], writes=["gate_bc"])
            else:
                bank5 = k.pb[5]
                for c in range(4):
                    P.tr(bank5[:, c * 128:(c + 1) * 128], modbc[:, c * 128:(c + 1) * 128], k.idf[:],
                         reads=["modbc", "idf"], writes=["bank5"])
                for c in range(4):
                    P.tr(bank[:, c * 128:(c + 1) * 128], modbc[:, (4 + c) * 128:(5 + c) * 128], k.idf[:],
                         reads=["modbc", "idf"], writes=["bank7"])
                for hb, bk, kn in ((0, bank5, "bank5"), (1, bank, "bank7")):
                    src = bk[:, :].rearrange("p (c f) -> p c f", c=4)[:, :, 0]
                    if v == 0:
                        P.copy("dve", k.B_col[:, hb * 4:(hb + 1) * 4], src, reads=[kn], writes=["B_col"])
                    else:
                        P.stt("dve", k.A_col[:, hb * 4:(hb + 1) * 4], src, 1.0, k.gcol[:, ls, hb * 4:(hb + 1) * 4],
                              ALU.add, ALU.mult, reads=[kn, ("gcol", ls)], writes=["A_col"])


def norm_mod(k):
    P, nc = k.P, k.nc
    with P.phase():
        k.junk = P.sbuf("junk", [128, D], BF16)
        k.xn = [P.sbuf(f"xn{i}", [128, D], BF16) for i in range(2)]
        norm_mod_inner(k)


def norm_mod_inner(k):
    P, nc = k.P, k.nc
    for t in range(NT):
        P.act(k.junk[:], k.X[:, t, :], AF.Square, accum_out=k.ssq[:, t:t + 1],
              reads=[("X", t)], writes=["junk", ("ssq", t)])
    P.ts("dve", k.rstd[:], k.ssq[:], 1.0 / D, EPS, ALU.mult, ALU.add, reads=[("ssq", t) for t in range(NT)],
         writes=["rstd"])
    P.tt("pool", k.rstd[:], k.rstd[:], k.mhalf[:].to_broadcast([128, NT]), ALU.pow, reads=["rstd", "mhalf"],
         writes=["rstd"])
    tp = k.tpb[:]
    for t in range(NT):
        xn = k.xn[t % 2]
        P.ts("dve", xn[:], k.X[:, t, :], k.rstd[:, t:t + 1], None, ALU.mult, reads=[("X", t), "rstd"],
             writes=[("xn", t % 2)])
        for c in range(8):
            P.tr(tp[:, c * 128:(c + 1) * 128], xn[:, c * 128:(c + 1) * 128], k.idb[:],
                 reads=[("xn", t % 2), "idb"], writes=["bank6"])
        dst = k.yT[:, :, t * 128:(t + 1) * 128]
        tpv = tp.rearrange("p (c f) -> p c f", c=8)
        P.tt("dve", dst, tpv, k.A_col[:].unsqueeze(2).to_broadcast([128, 8, 128]), ALU.mult,
             reads=["bank6", "A_col"], writes=[("yT", t)])
        P.tt("pool", dst, dst, k.B_col[:].unsqueeze(2).to_broadcast([128, 8, 128]), ALU.add,
             reads=[("yT", t), "B_col"], writes=[("yT", t)])


FFN_ROUNDS = [(0, 3), (3, 6), (6, 9), (9, 11)]


def ffn(k, layer, which):
    P, nc = k.P, k.nc
    with P.phase():
        ffn_inner(k, layer, which)


def ffn_inner(k, layer, which):
    P, nc = k.P, k.nc
    if True:
        k.ffn_bufs = dict(
            wa=[P.sbuf(f"wa{i}", [128, 8, 256], BF16) for i in range(2)],
            wb=[P.sbuf(f"wb{i}", [128, 8, 256], BF16) for i in range(2)],
            wo=[P.sbuf(f"wo{i}", [128, 6, D], BF16) for i in range(2)],
            gT=P.sbuf("gT", [128, 6, S], BF16),
            sa=[P.sbuf(f"sa{i}", [128, 512], F32) for i in range(2)],
            tmp=[P.sbuf(f"ftmp{i}", [128, 512], F32) for i in range(2)],
        )
        k.ffn_cnt = dict(blk=0, rnd=0, grp=0, ob=0)
    B = k.ffn_bufs
    C = k.ffn_cnt
    w_in = k.din["ffn_w_in"][layer, which]
    w_out = k.din["ffn_w_out"][layer, which]
    yT_reads = [("yT", t) for t in range(NT)]
    for (b0, b1) in FFN_ROUNDS:
        rb = C["rnd"] % 2
        C["rnd"] += 1
        wo = B["wo"][rb]
        r0 = b0 * 256
        r1 = min(b1 * 256, DFF)
        nfull = (r1 - r0) // 128
        P.dma("pool", wo[:, 0:nfull, :], w_out[r0:r0 + nfull * 128, :].rearrange("(c p) d -> p c d", p=128),
              writes=[("wo", rb)])
        rem = (r1 - r0) - nfull * 128
        if rem:
            P.dma("pool", wo[0:rem, nfull, :], w_out[r0 + nfull * 128:r1, :], writes=[("wo", rb)])
        chunks = []
        for blk in range(b0, b1):
            bb = C["blk"] % 2
            C["blk"] += 1
            col0 = blk * 256
            wdt = min(256, DFF - col0)
            wa, wb = B["wa"][bb], B["wb"][bb]
            P.dma("pool", wa[:, :, 0:wdt], w_in[:, col0:col0 + wdt].rearrange("(k p) c -> p k c", p=128),
                  writes=[("wa", bb)])
            P.dma("pool", wb[:, :, 0:wdt], w_in[:, DFF + col0:DFF + col0 + wdt].rearrange("(k p) c -> p k c", p=128),
                  writes=[("wb", bb)])
            for ch in range((wdt + 127) // 128):
                cw = min(128, wdt - ch * 128)
                ci = (blk - b0) * 2 + ch
                chunks.append((ci, cw))
                for n in range(4):
                    gb = C["grp"] % 2
                    C["grp"] += 1
                    pa, pbk = k.pb[gb], k.pb[2 + gb]
                    for kk in range(8):
                        P.mm(pa[0:cw, :], wa[:, kk, ch * 128:ch * 128 + cw], k.yT[:, kk, n * 512:(n + 1) * 512],
                             kk == 0, kk == 7, reads=[("wa", bb)] + yT_reads[4 * n:4 * n + 4], writes=[f"bank{gb}"])
                    for kk in range(8):
                        P.mm(pbk[0:cw, :], wb[:, kk, ch * 128:ch * 128 + cw], k.yT[:, kk, n * 512:(n + 1) * 512],
                             kk == 0, kk == 7, reads=[("wb", bb)] + yT_reads[4 * n:4 * n + 4], writes=[f"bank{2 + gb}"])
                    sa = B["sa"][gb]
                    P.act(sa[0:cw, :], pa[0:cw, :], AF.Silu, reads=[f"bank{gb}"], writes=[("sa", gb)])
                    P.tt("dve", B["gT"][0:cw, ci, n * 512:(n + 1) * 512], sa[0:cw, :], pbk[0:cw, :], ALU.mult,
                         reads=[("sa", gb), f"bank{2 + gb}"], writes=[("gT", ci, n)])
        for t in range(NT):
            for dh in range(2):
                ob = C["ob"] % 2
                C["ob"] += 1
                po = k.pb[4 + ob]
                for i, (ci, cw) in enumerate(chunks):
                    P.mm(po[:, :], B["gT"][0:cw, ci, t * 128:(t + 1) * 128], wo[0:cw, ci, dh * 512:(dh + 1) * 512],
                         i == 0, i == len(chunks) - 1, reads=[("gT", ci, t // 4), ("wo", rb)], writes=[f"bank{4 + ob}"])
                tmp = B["tmp"][ob]
                P.tt("dve", tmp[:], po[:, :], k.gate_bc[:, dh * 512:(dh + 1) * 512], ALU.mult,
                     reads=[f"bank{4 + ob}", "gate_bc"], writes=[("ftmp", ob)])
                xs = k.X[:, t, dh * 512:(dh + 1) * 512]
                P.tt("pool", xs, xs, tmp[:], ALU.add, reads=[("ftmp", ob), ("X", t)], writes=[("X", t)])


def even_mixer(k, j):
    P, nc = k.P, k.nc
    mixer_common(k)
    A = P.sbuf("A_ev", [128, NT, D], BF16)
    w_in = k.din["ev_w_in"][j]
    with P.phase():
        tiles = P.sbuf("btile1", [128, 1, 128], BF16)
        tile_of = {(0, 16, 0): 0}
        load_tile(k, tiles[:, 0, :], 0, 16, 0, ("btile", 0))
        cos = P.sbuf("cos", [128, NT, 16], F32)
        sin = P.sbuf("sin", [128, NT, 16], F32)
        P.dma("sp", cos[:], k.din["k_cos"].rearrange("(t p) f -> p t f", p=128), writes=["cos"])
        P.dma("sp", sin[:], k.din["k_sin"].rearrange("(t p) f -> p t f", p=128), writes=["sin"])
        k.rope_tmp = P.sbuf("rope_tmp", [128, 4, 8, 16], F32)

        def rope(dst, src, t, n, reads, writes):
            r = k.rope_tmp
            c = cos[:, t, :].unsqueeze(1).to_broadcast([128, n, 16])
            s = sin[:, t, :].unsqueeze(1).to_broadcast([128, n, 16])
            x1, x2 = src[:, :, 0:16], src[:, :, 16:32]
            P.tt("dve", r[:, 0, 0:n, :], x1, c, ALU.mult, reads=reads + ["cos"], writes=["rope_tmp"])
            P.tt("dve", r[:, 1, 0:n, :], x2, s, ALU.mult, reads=reads + ["sin"], writes=["rope_tmp"])
            P.tt("dve", dst[:, :, 0:16], r[:, 0, 0:n, :], r[:, 1, 0:n, :], ALU.subtract, reads=["rope_tmp"], writes=writes)
            P.tt("dve", r[:, 2, 0:n, :], x2, c, ALU.mult, reads=reads + ["cos"], writes=["rope_tmp2"])
            P.tt("dve", r[:, 3, 0:n, :], x1, s, ALU.mult, reads=reads + ["sin"], writes=["rope_tmp2"])
            P.tt("dve", dst[:, :, 16:32], r[:, 2, 0:n, :], r[:, 3, 0:n, :], ALU.add, reads=["rope_tmp2"], writes=writes)
        QT = P.sbuf("QTm", [128, 8, S], BF16)
        KT = P.sbuf("KTm", [128, 8, S], BF16)
        V = P.sbuf("Vm", [128, NT, 8, 65], BF16)
        P.memset("pool", V[:, :, :, 64:65], 1.0, writes=["Vones"])
        sc = 96 ** -0.5
        gql = load_gain(k, "gql", k.din["mla_q_norm_g"][j], 1.0, d=256)
        gkvl = load_gain(k, "gkvl", k.din["mla_kv_norm_g"][j], 1.0, d=128)
        gqn = load_gain(k, "gqn", k.din["mla_qk_g"][j, 0, 0:64], sc)
        gqr = load_gain(k, "gqr", k.din["mla_qk_g"][j, 0, 64:96], sc, d=32)
        gkn = load_gain(k, "gkn", k.din["mla_qk_g"][j, 1, 0:64], 1.0)
        gkr = load_gain(k, "gkr", k.din["mla_qk_g"][j, 1, 64:96], 1.0, d=32)
        with P.phase():
            wi = P.sbuf("wmi", [128, 8, 416], BF16)
            wuq = P.sbuf("wuq", [128, 2, 768], BF16)
            wukv = P.sbuf("wukv", [128, 1024], BF16)
            P.dma("pool", wi[:], w_in[:, 0:416].rearrange("(k p) c -> p k c", p=128), writes=["wmi"])
            P.dma("pool", wuq[:], k.din["mla_w_uq"][j].rearrange("(c p) n -> p c n", p=128), writes=["wuq"])
            P.dma("pool", wukv[:], k.din["mla_w_ukv"][j], writes=["wukv"])
            pj = [P.sbuf(f"pjm{i}", [128, 416], F32) for i in range(2)]
            latn = [P.sbuf(f"latn{i}", [128, 384], BF16) for i in range(2)]
            lat = [P.sbuf(f"lat{i}", [128, 3, 128], BF16) for i in range(2)]
            qf = [P.sbuf(f"qf{i}", [128, 8, 96], F32) for i in range(2)]
            Qst = [P.sbuf(f"Qst{i}", [128, 8, 96], BF16) for i in range(2)]
            Kst = [P.sbuf(f"Kst{i}", [128, 8, 96], BF16) for i in range(2)]
            krn = [P.sbuf(f"krn{i}", [128, 1, 32], F32) for i in range(2)]
            krr = [P.sbuf(f"krr{i}", [128, 1, 32], BF16) for i in range(2)]
            qrn = [P.sbuf(f"qrn{i}", [128, 8, 32], F32) for i in range(2)]
            tp = k.tpb[:]
            for t in range(NT):
                i2 = t % 2
                for kk in range(8):
                    P.mm(k.pb[3][:, 0:416], k.yT[:, kk, t * 128:(t + 1) * 128], wi[:, kk, :], kk == 0, kk == 7,
                         reads=[("yT", t), "wmi"], writes=["bank3"])
                P.copy("act", pj[i2][:], k.pb[3][:, 0:416], reads=["bank3"], writes=[("pjm", i2)])
                head_rms(k, latn[i2][:, 0:256].unsqueeze(1), pj[i2][:, 0:256].unsqueeze(1), 1, gql[:], [("pjm", i2)],
                         [("latn", i2)], "gql", d=256)
                head_rms(k, latn[i2][:, 256:384].unsqueeze(1), pj[i2][:, 256:384].unsqueeze(1), 1, gkvl[:], [("pjm", i2)],
                         [("latn", i2)], "gkvl", d=128)
                head_rms(k, krn[i2][:], pj[i2][:, 384:416].unsqueeze(1), 1, gkr[:], [("pjm", i2)], [("krn", i2)], "gkr", d=32)
                rope(krr[i2][:], krn[i2][:], t, 1, [("krn", i2)], [("krr", i2)])
                for c in range(3):
                    P.tr(tp[:, c * 128:(c + 1) * 128], latn[i2][:, c * 128:(c + 1) * 128], k.idb[:],
                         reads=[("latn", i2), "idb"], writes=["bank6"])
                P.copy("act", lat[i2][:], tp[:, 0:384].rearrange("p (c f) -> p c f", c=3), reads=["bank6"],
                       writes=[("lat", i2)])
                for (bank, bkey, c0, cw) in ((k.pb[0], "bank0", 0, 512), (k.pb[1], "bank1", 512, 256)):
                    for c in range(2):
                        P.mm(bank[:, 0:cw], lat[i2][:, c, :], wuq[:, c, c0:c0 + cw], c == 0, c == 1,
                             reads=[("lat", i2), "wuq"], writes=[bkey])
                qff = qf[i2][:].rearrange("p h x -> p (h x)")
                P.copy("act", qff[:, 0:512], k.pb[0][:, 0:512], reads=["bank0"], writes=[("qf", i2)])
                P.copy("act", qff[:, 512:768], k.pb[1][:, 0:256], reads=["bank1"], writes=[("qf", i2)])
                head_rms(k, Qst[i2][:, :, 0:64], qf[i2][:, :, 0:64], 8, gqn[:], [("qf", i2)], [("Qst", i2)], "gqn")
                head_rms(k, qrn[i2][:], qf[i2][:, :, 64:96], 8, gqr[:], [("qf", i2)], [("qrn", i2)], "gqr", d=32)
                rope(Qst[i2][:, :, 64:96], qrn[i2][:], t, 8, [("qrn", i2)], [("Qst", i2)])
                for (bank, bkey, c0) in ((k.pb[2], "bank2", 0), (k.pb[4], "bank4", 512)):
                    P.mm(bank[:, :], lat[i2][:, 2, :], wukv[:, c0:c0 + 512], True, True, reads=[("lat", i2), "wukv"],
                         writes=[bkey])
                for hb, (bank, bkey) in enumerate(((k.pb[2], "bank2"), (k.pb[4], "bank4"))):
                    kvv = bank[:, :].rearrange("p (h x) -> p h x", x=128)
                    head_rms(k, Kst[i2][:, hb * 4:(hb + 1) * 4, 0:64], kvv[:, :, 0:64], 4, gkn[:], [bkey], [("Kst", i2)], "gkn")
                    P.copy("act", V[:, t, hb * 4:(hb + 1) * 4, 0:64], kvv[:, :, 64:128], reads=[bkey], writes=[("V", t)])
                P.copy("pool", Kst[i2][:, :, 64:96], krr[i2][:].to_broadcast([128, 8, 32]), reads=[("krr", i2)],
                       writes=[("Kst", i2)])
                for (st, dstT, nm) in ((Qst[i2], QT, "QT"), (Kst[i2], KT, "KT")):
                    for h in range(8):
                        P.tr(tp[0:96, h * 128:(h + 1) * 128], st[:, h, :], k.idb[:], reads=[(nm[0] + "st", i2), "idb"],
                             writes=["bank6"])
                    P.copy("dve", dstT[0:96, :, t * 128:(t + 1) * 128], tp[0:96, :].rearrange("p (h f) -> p h f", h=8),
                           reads=["bank6"], writes=[(nm, t)])
        allQK = [("QT", t) for t in range(NT)] + [("KT", t) for t in range(NT)]
        allV = [("V", t) for t in range(NT)] + ["Vones"]
        mi = tile_of[(0, 16, 0)]
        nh_mla = 0 if k.stages == "A" else 8

        def mla_bias(delta):
            if delta < 0:
                return "skip"
            if delta == 0:
                return (tiles[:, mi, :], ("btile", mi))
            return None
        for h in range(nh_mla):
            comp = dict(QT=lambda g, h=h: QT[0:96, h, g * 512:(g + 1) * 512],
                        KT=lambda kt, h=h: KT[0:96, h, kt * 128:(kt + 1) * 128],
                        V=lambda kt, h=h: V[:, kt, h, :], bias=mla_bias, cbias=None, nkt=NT, ksz=128,
                        reads=allQK, vreads=allV, extra=None)
            attend(k, [comp], 65, norm_consume(k, lambda g, h=h: (A[:, 4 * g:4 * g + 4, h * 64:(h + 1) * 64],
                                                                  [("A", 4 * g + q) for q in range(4)])), f"mla{h}")
    if k.stages in ("A", "B"):
        build_AT(k, A, 8)
        return
    with P.phase():
        kcT = P.sbuf("kcT", [128, 2, 128], BF16)
        Vc = P.sbuf("Vc", [128, 2, 97], BF16)
        gnq = load_gain(k, "gnq", k.din["nsa_qk_g"][j, 0], 0.125)
        gnk = load_gain(k, "gnk", k.din["nsa_qk_g"][j, 1], 1.0)
        with P.phase():
            wc = P.sbuf("wc", [128, 8, 256], BF16)
            P.dma("pool", wc[:], w_in[:, EV["kc"]:EV["kc"] + 256].rearrange("(k p) c -> p k c", p=128), writes=["wc"])
            W1 = P.sbuf("W1", [64, 2, 32, 256], BF16)
            W2 = P.sbuf("W2", [128, 2, 2, 64], BF16)
            for jj in range(2):
                for lh in range(2):
                    P.dma("pool", W1[:, jj, lh * 16:(lh + 1) * 16, :],
                          k.din["nsa_cmp_w1"][j, jj, lh * 1024:(lh + 1) * 1024, :].rearrange("(l d) h -> d l h", d=64),
                          writes=["W1"])
                P.dma("pool", W2[:, jj, :, :], k.din["nsa_cmp_w2"][j, jj].rearrange("(c p) d -> p c d", p=128), writes=["W2"])
            petok = P.sbuf("petok", [128, 2, 2, 64], F32)
            for jj in range(2):
                for var in range(2):
                    for rep in range(8):
                        P.dma("sp", petok[rep * 16:(rep + 1) * 16, jj, var, :],
                              k.din["nsa_cmp_pe"][j, jj, var * 16:(var + 1) * 16, :], writes=["petok"])
            cT = P.sbuf("cT_cmp", [64, 8, S], BF16)
            stc = [P.sbuf(f"stc{i}", [128, 8, 64], BF16) for i in range(2)]
            tp = k.tpb[:]
            for t in range(NT):
                i2 = t % 2
                for kk in range(8):
                    P.mm(k.pb[3][:, 0:256], k.yT[:, kk, t * 128:(t + 1) * 128], wc[:, kk, :], kk == 0, kk == 7,
                         reads=[("yT", t), "wc"], writes=["bank3"])
                src = k.pb[3][:, 0:256].rearrange("p (j g d) -> p j g d", j=2, g=2)
                for jj in range(2):
                    for var in range(2):
                        P.tt("dve", stc[i2][:, jj * 4 + var * 2: jj * 4 + var * 2 + 2, :], src[:, jj, :, :],
                             petok[:, jj, var, :].unsqueeze(1).to_broadcast([128, 2, 64]), ALU.add,
                             reads=["bank3", "petok"], writes=[("stc", i2)])
                for i in range(8):
                    P.tr(tp[0:64, i * 128:(i + 1) * 128], stc[i2][:, i, :], k.idb[:], reads=[("stc", i2), "idb"],
                         writes=["bank6"])
                P.copy("act", cT[:, :, t * 128:(t + 1) * 128], tp[0:64, :].rearrange("p (i f) -> p i f", i=8),
                       reads=["bank6"], writes=[("cT", t)])
            allcT = [("cT", t) for t in range(NT)]
            hT = P.sbuf("hT", [128, 2, 2, 254], BF16)
            for jj in range(2):
                for hc in range(2):
                    bank = k.pb[hc]
                    for l in range(32):
                        var = l // 16
                        i0_ = jj * 4 + var * 2
                        rhs = cT[:, i0_:i0_ + 2, l: l + 16 * 126 + 1: 16]
                        P.mm(bank[:, 0:254].rearrange("p (g n) -> p g n", g=2), W1[:, jj, l, hc * 128:(hc + 1) * 128], rhs,
                             l == 0, l == 31, reads=allcT + ["W1"], writes=[f"bank{hc}"])
                    P.act(hT[:, jj, hc, :], bank[:, 0:254], AF.Silu, reads=[f"bank{hc}"], writes=["hT"])
            kraw = P.sbuf("kraw", [128, 2, 2, 64], BF16)
            P.memset("dve", kraw[:], 0.0, writes=["kraw"])
            P.memset("dve", Vc[:], 0.0, writes=["Vc"])
            ovl = P.sbuf("ovl", [128, 32], F32)
            P.dma("sp", ovl[0:127, :], k.din["k_overlap"], writes=["ovl"])
            for g in range(2):
                for jj in range(2):
                    bank = k.pb[2 + jj]
                    for hc in range(2):
                        P.mm(bank[0:127, 0:64], hT[:, jj, hc, g * 127:(g + 1) * 127], W2[:, jj, hc, :], hc == 0, hc == 1,
                             reads=["hT", "W2"], writes=[f"bank{2 + jj}"])
                head_rms(k, kraw[0:127, g, 0:1, :], k.pb[2][0:127, 0:64].unsqueeze(1), 1, gnk[0:127, :], ["bank2"], ["kraw"],
                         "gnk", npart=127)
                P.copy("dve", kraw[0:127, g, 1, :], kraw[0:127, g, 0, :], reads=["kraw"], writes=["kraw"])
                P.copy("act", Vc[0:127, g, 0:64], k.pb[3][0:127, 0:64], reads=["bank3"], writes=["Vc"])
                P.memset("dve", Vc[0:127, g, 64:65], 1.0, writes=["Vc"])
                P.copy("dve", Vc[0:127, g, 65:97], ovl[0:127, :], reads=["ovl", "Vc"], writes=["Vc"])
                P.tr(tp[:, g * 128:(g + 1) * 128], kraw[:, g, :, :].rearrange("p a d -> p (a d)"), k.idb[:],
                     reads=["kraw", "idb"], writes=["bank6"])
            P.copy("dve", kcT[:], tp[:, 0:256].rearrange("p (g f) -> p g f", g=2), reads=["bank6"], writes=["kcT"])
        if k.stages == "C":
            return
        st = k.stages or ""
        QT = P.sbuf("QTn", [128, 4, S], BF16)
        KsT = P.sbuf("KsT", [128, 2, S], BF16)
        KwT = P.sbuf("KwT", [128, 2, S], BF16)
        Vs = P.sbuf("Vs", [128, NT, 2, 65], BF16)
        Vw = P.sbuf("Vw", [128, NT, 2, 65], BF16)
        gates = P.sbuf("gates", [128, NT, 24], F32)
        FV = P.sbuf("FV", [128, NT, 32], F32)
        P.memset("pool", Vs[:, :, :, 64:65], 1.0, writes=["Vsones"])
        P.memset("pool", Vw[:, :, :, 64:65], 1.0, writes=["Vwones"])
        with P.phase():
            wq = P.sbuf("wq", [128, 8, 512], BF16)
            wr = P.sbuf("wr", [128, 8, 512], BF16)
            wg = P.sbuf("wg", [128, 8, 32], BF16)
            P.dma("pool", wq[:], w_in[:, EV["qn"]:EV["qn"] + 512].rearrange("(k p) c -> p k c", p=128), writes=["wq"])
            P.dma("pool", wr[:], w_in[:, EV["ks"]:EV["ks"] + 512].rearrange("(k p) c -> p k c", p=128), writes=["wr"])
            P.dma("pool", wg[:], w_in[:, EV["end"] - 32:EV["end"]].rearrange("(k p) c -> p k c", p=128), writes=["wr"])
            fvt = P.sbuf("fvt", [128, NT, 32], F32)
            P.dma("sp", FV[:], k.din["k_force"].rearrange("(t p) f -> p t f", p=128), writes=["FV"])
            P.dma("sp", fvt[:], k.din["k_valid"].rearrange("(t p) f -> p t f", p=128), writes=["fvt"])
            P.tt("dve", FV[:], FV[:], fvt[:], ALU.add, reads=["FV", "fvt"], writes=["FV"])
            Qst = [P.sbuf(f"Qsn{i}", [128, 8, 64], BF16) for i in range(2)]
            Kst = [P.sbuf(f"Ksn{i}", [128, 2, 2, 2, 64], BF16) for i in range(2)]
            tp = k.tpb[:]
            for t in range(NT if st != "D1" else 0):
                i2 = t % 2
                for (bank, bkey, w, cw, c0) in ((k.pb[0], "bank0", wq, 512, 0), (k.pb[1], "bank1", wr, 512, 0),
                                                (k.pb[2], "bank2", wg, 32, 0)):
                    for kk in range(8):
                        P.mm(bank[:, 0:cw], k.yT[:, kk, t * 128:(t + 1) * 128], w[:, kk, c0:c0 + cw], kk == 0, kk == 7,
                             reads=[("yT", t), "wq", "wr"], writes=[bkey])
                head_rms(k, Qst[i2][:], k.pb[0][:, :].rearrange("p (h d) -> p h d", d=64), 8, gnq[:], ["bank0"],
                         [("Qsn", i2)], "gnq")
                r4 = k.pb[1][:, :].rearrange("p (a g d) -> p a g d", a=4, g=2)
                for a, ai in ((0, 0), (2, 1)):
                    head_rms(k, Kst[i2][:, ai, :, 0, :], r4[:, a, :, :], 2, gnk[:], ["bank1"], [("Ksn", i2)], "gnk")
                    P.copy("pool", Kst[i2][:, ai, :, 1, :], Kst[i2][:, ai, :, 0, :], reads=[("Ksn", i2)], writes=[("Ksn", i2)])
                P.copy("act", Vs[:, t, :, 0:64], r4[:, 1, :, :], reads=["bank1"], writes=[("Vs", t)])
                P.copy("act", Vw[:, t, :, 0:64], r4[:, 3, :, :], reads=["bank1"], writes=[("Vw", t)])
                if st != "D2":
                    P.act(gates[:, t, :], k.pb[2][:, 8:32], AF.Sigmoid, reads=["bank2"], writes=[("gates", t)])
                for pr in range(4):
                    P.tr(tp[:, pr * 128:(pr + 1) * 128], Qst[i2][:, 2 * pr:2 * pr + 2, :].rearrange("p a d -> p (a d)"),
                         k.idb[:], reads=[("Qsn", i2), "idb"], writes=["bank6"])
                for ai in range(2):
                    for g in range(2):
                        c = 4 + ai * 2 + g
                        P.tr(tp[:, c * 128:(c + 1) * 128], Kst[i2][:, ai, g, :, :].rearrange("p a d -> p (a d)"), k.idb[:],
                             reads=[("Ksn", i2), "idb"], writes=["bank6"])
                P.copy("dve", QT[:, :, t * 128:(t + 1) * 128], tp[:, 0:512].rearrange("p (c f) -> p c f", c=4),
                       reads=["bank6"], writes=[("QT", t)])
                P.copy("act", KsT[:, :, t * 128:(t + 1) * 128], tp[:, 512:768].rearrange("p (c f) -> p c f", c=2),
                       reads=["bank6"], writes=[("KsT", t)])
                P.copy("act", KwT[:, :, t * 128:(t + 1) * 128], tp[:, 768:1024].rearrange("p (c f) -> p c f", c=2),
                       reads=["bank6"], writes=[("KwT", t)])
        if st.startswith("D"):
            return
        Anf = P.sbuf("Anf", [128, NT, 256], F32)
        imp = P.sbuf("imp", [128, NT, 32], F32)
        selT = P.sbuf("selT", [32, S], BF16)
        E32 = P.sbuf("E32", [32, S], BF16)
        P.dma("pool", E32[:].rearrange("r (a c) -> r a c", a=4), k.din["k_exp32"].rearrange("r (a c) -> r a c", a=4),
              writes=["E32"])
        tiles = P.sbuf("btiles", [128, 40, 128], BF16)
        tile_of = {}
        specs = ([(0, h, d) for h in range(8, 16) for d in (0, 1)] + [(1, h, d) for h in range(8, 16) for d in (0, 1, 4)])
        for i, (v, h, d) in enumerate(specs):
            tile_of[(v, h, d)] = i
            load_tile(k, tiles[:, i, :], v, h, d, ("btile", i))
        allQ = [("QT", t) for t in range(NT)]
        gk = [("gates", t) for t in range(NT)]
        wts = [P.sbuf(f"wts{i}", [128, 4], F32) for i in range(2)]
        otmp = [P.sbuf(f"otmp{i}", [128, 4, 64], F32) for i in range(2)]
        cmpb = P.sbuf("cmpb", [128, S], BF16)
        sc_ = P.sbuf("scn", [128, 32], F32)
        mx = P.sbuf("mxn", [128, 8], F32)
        sel = P.sbuf("seln", [128, 32], F32)
        val = P.sbuf("valn", [128, 32], F32)
        selb = P.sbuf("selbn", [128, 32], BF16)
        cn = dict(n=0)

        def nsa_consume(h, br, with_imp):
            hl = h % 4

            def consume(g, ov, okey):
                i2 = cn["n"] % 2
                cn["n"] += 1
                r, rk = k.rden[i2], ("rden", i2)
                P.ts("dve", r[:], ov[:, :, 64], 1e-30, None, ALU.max, reads=[okey], writes=[rk])
                P.op("dve", lambda: nc.vector.reciprocal(out=r[:], in_=r[:]), reads=[rk], writes=[rk])
                w, wk = wts[i2], ("wts", i2)
                P.tt("dve", w[:], r[:], gates[:, 4 * g:4 * g + 4, h * 3 + br], ALU.mult, reads=[rk] + gk, writes=[wk])
                dst = Anf[:, 4 * g:4 * g + 4, hl * 64:(hl + 1) * 64]
                akeys = [("Anf", 4 * g + q) for q in range(4)]
                wb = w[:].unsqueeze(2).to_broadcast([128, 4, 64])
                if br == 0:
                    P.tt("dve", dst, ov[:, :, 0:64], wb, ALU.mult, reads=[okey, wk], writes=akeys)
                else:
                    o, ok_ = otmp[i2], ("otmp", i2)
                    P.tt("dve", o[:], ov[:, :, 0:64], wb, ALU.mult, reads=[okey, wk], writes=[ok_])
                    P.tt("pool", dst, dst, o[:], ALU.add, reads=[ok_] + akeys, writes=akeys)
                if with_imp:
                    idst = imp[:, 4 * g:4 * g + 4, :]
                    ikeys = [("imp", 4 * g + q) for q in range(4)]
                    rb = r[:].unsqueeze(2).to_broadcast([128, 4, 32])
                    if hl == 0:
                        P.tt("dve", idst, ov[:, :, 65:97], rb, ALU.mult, reads=[okey, rk], writes=ikeys)
                    else:
                        o, ok_ = otmp[i2], ("otmp", i2)
                        P.tt("dve", o[:, :, 0:32], ov[:, :, 65:97], rb, ALU.mult, reads=[okey, rk], writes=[ok_])
                        P.tt("pool", idst, idst, o[:, :, 0:32], ALU.add, reads=[ok_] + ikeys, writes=ikeys)
            return consume

        def qloc(h):
            half, pos = h % 2, h // 2
            return half * 64, half * 64 + 64, pos

        def nsa_bias(v, h, kind):
            def f(delta):
                if delta < 0:
                    return "skip"
                if delta <= 1:
                    i = tile_of[(v, 8 + h, delta)]
                    return (tiles[:, i, :], ("btile", i))
                if kind == "sel":
                    return None
                if delta <= 3:
                    return None
                if delta == 4:
                    i = tile_of[(v, 8 + h, 4)]
                    return (tiles[:, i, :], ("btile", i))
                return "skip"
            return f
        allsel = [("selT", t) for t in range(NT)]
        for grp in range(2):
            for h in range(grp * 4, grp * 4 + 4):
                r0, r1, pos = qloc(h)
                src = bass.AP(tensor=k.shc.tensor, offset=h * 127 * (LC + 16) + 2016, ap=[[LC, 127], [1, S]])
                P.dma("sp", cmpb[0:127, :], src, reads=[("shc", 8 + h)], writes=["cmpb"])
                comp = dict(QT=lambda g, r0=r0, r1=r1, pos=pos: QT[r0:r1, pos, g * 512:(g + 1) * 512],
                            KT=lambda kt, r0=r0, r1=r1, grp=grp: kcT[r0:r1, grp, 0:127],
                            V=lambda kt, grp=grp: Vc[0:127, grp, :],
                            bias=lambda delta: (cmpb[0:127, delta * 128:(delta + 1) * 128], "cmpb"),
                            cbias=k.c31[:, 8 + h:9 + h], nkt=1, ksz=127, reads=allQ + ["kcT"], vreads=["Vc"], extra=None)
                attend(k, [comp], 97, nsa_consume(h, 0, True), f"cmp{h}")
            for t in range(NT):
                P.tt("dve", sc_[:], imp[:, t, :], FV[:, t, :], ALU.add, reads=[("imp", t), "FV"], writes=["scn"])
                P.op("dve", lambda: nc.vector.max(out=mx[:], in_=sc_[:]), reads=["scn"], writes=["mxn"])
                P.ts("dve", sel[:], sc_[:], mx[:, 7:8], None, ALU.is_ge, reads=["scn", "mxn"], writes=["seln"])
                P.ts("dve", val[:], sc_[:], -1e29, None, ALU.is_gt, reads=["scn"], writes=["valn"])
                P.tt("dve", sel[:], sel[:], val[:], ALU.mult, reads=["seln", "valn"], writes=["seln"])
                P.ts("dve", selb[:], sel[:], -NEG, NEG, ALU.mult, ALU.add, reads=["seln"], writes=["selbn"])
                P.tr(k.tpb[0:32, 0:128], selb[:, :], k.idb[:], reads=["selbn", "idb"], writes=["bank6"])
                P.copy("dve", selT[:, t * 128:(t + 1) * 128], k.tpb[0:32, 0:128], reads=["bank6"], writes=[("selT", t)])
            for h in range(grp * 4, grp * 4 + 4):
                r0, r1, pos = qloc(h)
                base = dict(QT=lambda g, r0=r0, r1=r1, pos=pos: QT[r0:r1, pos, g * 512:(g + 1) * 512],
                            cbias=k.c31[:, 8 + h:9 + h], nkt=NT, ksz=128)
                comp_s = dict(base, KT=lambda kt, r0=r0, r1=r1, grp=grp: KsT[r0:r1, grp, kt * 128:(kt + 1) * 128],
                              V=lambda kt, grp=grp: Vs[:, kt, grp, :], bias=nsa_bias(0, h, "sel"),
                              reads=allQ + [("KsT", t) for t in range(NT)],
                              vreads=[("Vs", t) for t in range(NT)] + ["Vsones"],
                              extra=(lambda kt: E32[:, kt * 128:(kt + 1) * 128],
                                     lambda g: selT[:, g * 512:(g + 1) * 512]), extra_reads=["E32"] + allsel)
                attend(k, [comp_s], 65, nsa_consume(h, 1, False), f"sel{h}")
                comp_w = dict(base, KT=lambda kt, r0=r0, r1=r1, grp=grp: KwT[r0:r1, grp, kt * 128:(kt + 1) * 128],
                              V=lambda kt, grp=grp: Vw[:, kt, grp, :], bias=nsa_bias(1, h, "win"),
                              reads=allQ + [("KwT", t) for t in range(NT)],
                              vreads=[("Vw", t) for t in range(NT)] + ["Vwones"], extra=None)
                attend(k, [comp_w], 65, nsa_consume(h, 2, False), f"win{h}")
            for t in range(NT):
                P.copy("act", A[:, t, 512 + grp * 256:512 + (grp + 1) * 256], Anf[:, t, :], reads=[("Anf", t)],
                       writes=[("A", t)])
    build_AT(k, A, 8)


def mixer_common(k):
    P = k.P
    k.rms_sq = P.sbuf("rms_sq", [128, 1152], F32)
    k.rms_ss = P.sbuf("rms_ss", [128, 16], F32)
    k.att_state = dict(o=0, s=0, p=0, pbuf=[P.sbuf(f"pbuf{i}", [128, 512], BF16) for i in range(3)])
    k.rden = [P.sbuf(f"rden{i}", [128, 4], F32) for i in range(2)]
    k.rden_n = 0


def load_gain(k, name, src, scale, d=64):
    P = k.P
    g = P.sbuf(name, [128, d], F32)
    P.dma("sp", g[:], src.partition_broadcast(128), writes=[name])
    if scale != 1.0:
        P.ts("dve", g[:], g[:], scale, None, ALU.mult, reads=[name], writes=[name])
    return g


def norm_consume(k, dst_fn, okey_reads=()):
    P, nc = k.P, k.nc

    def consume(g, ov, okey):
        r = k.rden[k.rden_n % 2]
        rk = ("rden", k.rden_n % 2)
        k.rden_n += 1
        P.ts("dve", r[:], ov[:, :, 64], 1e-30, None, ALU.max, reads=[okey], writes=[rk])
        P.op("dve", lambda: nc.vector.reciprocal(out=r[:], in_=r[:]), reads=[rk], writes=[rk])
        dst, dkeys = dst_fn(g)
        P.tt("dve", dst, ov[:, :, 0:64], r[:].unsqueeze(2).to_broadcast([128, 4, 64]), ALU.mult,
             reads=[okey, rk], writes=dkeys)
    return consume


def odd_mixer(k, j):
    P, nc = k.P, k.nc
    w_in = k.din["od_w_in"][j]
    mixer_common(k)
    QT = P.sbuf("QT", [128, 8, S], BF16)
    KT = P.sbuf("KT", [128, 8, S], BF16)
    V = P.sbuf("V16", [128, NT, 16, 65], BF16)
    A = P.sbuf("A_odd", [128, NT, 512], BF16)
    tiles = P.sbuf("btiles", [128, 48, 128], BF16)
    gdq = load_gain(k, "gdq", k.din["dil_qk_g"][j, 0], 0.125)
    gdk = load_gain(k, "gdk", k.din["dil_qk_g"][j, 1], 1.0)
    gmq = load_gain(k, "gmq", k.din["moba_qk_g"][j, 0], 0.125)
    gmk = load_gain(k, "gmk", k.din["moba_qk_g"][j, 1], 1.0)
    tile_of = {}
    specs = ([(2, s, d) for s in range(0, 4) for d in (0, 1)] + [(3, s, d) for s in range(4, 8) for d in range(5)]
             + [(4, s, d) for s in range(8, 12) for d in (0, 1, 2)] + [(0, s, d) for s in range(12, 16) for d in (0, 1)])
    for i, (v, s, d) in enumerate(specs):
        tile_of[(v, s, d)] = i
        load_tile(k, tiles[:, i, :], v, s, d, ("btile", i))
    P.memset("pool", V[:, :, :, 64:65], 1.0, writes=["Vones"])

    def slot_loc(s):
        if s < 12:
            return s // 6, s % 6
        h = s - 12
        return h // 2, 6 + h % 2

    with P.phase():
        wbuf = [P.sbuf(f"wod{i}", [128, 8, 768], BF16) for i in range(2)]
        proj = [P.sbuf(f"proj{i}", [128, 768], F32) for i in range(2)]
        stg = [P.sbuf(f"stg{i}", [128, 6, 2, 64], BF16) for i in range(2)]
        n = 0
        for gi in range(4):
            w = wbuf[gi % 2]
            P.dma("pool", w[:], w_in[:, gi * 768:(gi + 1) * 768].rearrange("(k p) c -> p k c", p=128),
                  writes=[("wod", gi % 2)])
            for t in range(NT):
                pj = proj[n % 2]
                pk = ("proj", n % 2)
                for (bank, bkey, c0, cw) in ((k.pb[3], "bank3", 0, 512), (k.pb[7], "bank7", 512, 256)):
                    for kk in range(8):
                        P.mm(bank[:, 0:cw], k.yT[:, kk, t * 128:(t + 1) * 128], w[:, kk, c0:c0 + cw], kk == 0, kk == 7,
                             reads=[("yT", t), ("wod", gi % 2)], writes=[bkey])
                    if gi == 2:
                        s0, ns = (0, 8) if c0 == 0 else (8, 4)
                        P.copy("act", V[:, t, s0:s0 + ns, 0:64], bank[:, 0:cw].rearrange("p (s d) -> p s d", d=64),
                               reads=[bkey], writes=[("V", t)])
                    else:
                        P.copy("act", pj[:, c0:c0 + cw], bank[:, 0:cw], reads=[bkey], writes=[pk])
                if gi == 2:
                    continue
                tp = k.tpb[:]
                if gi < 2:
                    st = stg[n % 2]
                    sk = ("stg", n % 2)
                    head_rms(k, st[:].rearrange("p s h d -> p h s d"), pj[:].rearrange("p (s d) -> p s d", d=64), 12,
                             (gdq if gi == 0 else gdk)[:], [pk], [sk], "gdq" if gi == 0 else "gdk", split=(2, 6))
                    for pr in range(6):
                        P.tr(tp[:, pr * 128:(pr + 1) * 128], st[:, pr, :, :].rearrange("p h d -> p (h d)"), k.idb[:],
                             reads=[sk, "idb"], writes=["bank6"])
                    dstT = (QT if gi == 0 else KT)
                    P.copy("dve", dstT[:, 0:6, t * 128:(t + 1) * 128], tp[:, 0:768].rearrange("p (c f) -> p c f", c=6),
                           reads=["bank6"], writes=[("QT" if gi == 0 else "KT", t)])
                else:
                    for qk in range(2):
                        st = stg[(n + qk) % 2]
                        sk = ("stg", (n + qk) % 2)
                        head_rms(k, st[:, 0:2, :, :].rearrange("p s h d -> p h s d"),
                                 pj[:, qk * 256:(qk + 1) * 256].rearrange("p (s d) -> p s d", d=64), 4,
                                 (gmq if qk == 0 else gmk)[:], [pk], [sk], "gmq" if qk == 0 else "gmk", split=(2, 2))
                        for pr in range(2):
                            P.tr(tp[:, pr * 128:(pr + 1) * 128], st[:, pr, :, :].rearrange("p h d -> p (h d)"), k.idb[:],
                                 reads=[sk, "idb"], writes=["bank6"])
                        dstT = (QT if qk == 0 else KT)
                        P.copy("dve", dstT[:, 6:8, t * 128:(t + 1) * 128], tp[:, 0:256].rearrange("p (c f) -> p c f", c=2),
                               reads=["bank6"], writes=[("QTm" if qk == 0 else "KTm", t)])
                    P.copy("act", V[:, t, 12:16, 0:64], pj[:, 512:768].rearrange("p (s d) -> p s d", d=64),
                           reads=[pk], writes=[("Vm", t)])
                n += 1
    allQ = [("QT", t) for t in range(NT)] + [("QTm", t) for t in range(NT)]
    allK = [("KT", t) for t in range(NT)] + [("KTm", t) for t in range(NT)]
    allV = [("V", t) for t in range(NT)] + [("Vm", t) for t in range(NT)] + ["Vones"]

    def mk_bias(v, s, kind):
        def f(delta):
            if delta < 0:
                return "skip"
            if kind == "d0":
                dd = delta if delta <= 1 else None
            elif kind == "d1":
                dd = delta if delta <= 4 else None
            elif kind == "d2":
                dd = min(delta, 2)
            else:
                if delta >= 2:
                    return None
                dd = delta
            if dd is None:
                return "skip"
            i = tile_of[(v, s, dd)]
            return (tiles[:, i, :], ("btile", i))
        return f

    def mk_comp(s, bias_fn, extra=None, extra_reads=()):
        half, pos = slot_loc(s)
        r0, r1 = half * 64, half * 64 + 64
        return dict(QT=lambda g: QT[r0:r1, pos, g * 512:(g + 1) * 512], KT=lambda kt: KT[r0:r1, pos, kt * 128:(kt + 1) * 128],
                    V=lambda kt: V[:, kt, s, :], bias=bias_fn, cbias=k.c31[:, s:s + 1], nkt=NT, ksz=128,
                    reads=allQ + allK, vreads=allV, extra=extra, extra_reads=list(extra_reads))

    for h in range(4):
        comps = [mk_comp(h, mk_bias(2, h, "d0")), mk_comp(4 + h, mk_bias(3, 4 + h, "d1")),
                 mk_comp(8 + h, mk_bias(4, 8 + h, "d2"))]
        attend(k, comps, 65, norm_consume(k, lambda g, h=h: (A[:, 4 * g:4 * g + 4, h * 64:(h + 1) * 64],
                                                              [("A", 4 * g + q) for q in range(4)])), f"dil{h}")
    with P.phase():
        kmT = P.sbuf("kmT", [128, 2, 32], BF16)
        kmf = P.sbuf("kmf", [128, 2, 8], F32)
        E8 = P.sbuf("E8", [32, S], BF16)
        E8f = P.sbuf("E8f", [32, S], F32)
        selT = [P.sbuf(f"selT{i}", [32, S], BF16) for i in range(2)]
        sc = P.sbuf("sc", [128, 8], F32)
        mx = P.sbuf("mx", [128, 8], F32)
        sel = P.sbuf("sel", [128, 8], F32)
        val = P.sbuf("val", [128, 8], F32)
        selb = P.sbuf("selb", [128, 32], BF16)
        P.memset("dve", E8f[:], 0.0, writes=["E8f"])
        P.dma("sp", E8f[0:8, :], k.din["k_exp8"], writes=["E8f"])
        P.copy("dve", E8[:], E8f[:], reads=["E8f"], writes=["E8"])
        P.memset("dve", kmT[:], 0.0, writes=["kmT"])
        P.memset("dve", selb[:], 0.0, writes=["selb"])
        P.reduce("dve", kmf[:], KT[:, 6:8, :].rearrange("p c (b f) -> p c b f", b=8), ALU.add, reads=allK, writes=["kmf"])
        P.ts("dve", kmT[:, :, 0:8], kmf[:], 1.0 / 256, None, ALU.mult, reads=["kmf"], writes=["kmT"])
        for h in range(4):
            half, pos = slot_loc(12 + h)
            r0, r1 = half * 64, half * 64 + 64
            sT = selT[h % 2]
            stk = ("selT", h % 2)
            for t in range(NT):
                own = t // 2
                P.mm(k.pb[7][:, 0:32], QT[r0:r1, pos, t * 128:(t + 1) * 128], kmT[r0:r1, pos - 6, :], True,
                     reads=allQ + ["kmT"], writes=["bank7"])
                P.copy("dve", sc[:], k.pb[7][:, 0:8], reads=["bank7"], writes=["sc"])
                P.memset("dve", sc[:, own:8], -1e30, writes=["sc"])
                P.op("dve", lambda: nc.vector.max(out=mx[:], in_=sc[:]), reads=["sc"], writes=["mx"])
                P.ts("dve", sel[:], sc[:], mx[:, 2:3], None, ALU.is_ge, reads=["sc", "mx"], writes=["sel"])
                P.ts("dve", val[:], sc[:], -1e29, None, ALU.is_gt, reads=["sc"], writes=["val"])
                P.tt("dve", sel[:], sel[:], val[:], ALU.mult, reads=["sel", "val"], writes=["sel"])
                P.memset("dve", sel[:, own:own + 1], 1.0, writes=["sel"])
                P.ts("dve", selb[:, 0:8], sel[:], -NEG, NEG, ALU.mult, ALU.add, reads=["sel"], writes=["selb"])
                P.tr(k.tpb[0:32, 0:128], selb[:, :], k.idb[:], reads=["selb", "idb"], writes=["bank6"])
                P.copy("dve", sT[:, t * 128:(t + 1) * 128], k.tpb[0:32, 0:128], reads=["bank6"], writes=[stk])
            comp = mk_comp(12 + h, mk_bias(0, 12 + h, "moba"),
                           extra=(lambda kt: E8[:, kt * 128:(kt + 1) * 128], lambda g, sT=sT: sT[:, g * 512:(g + 1) * 512]),
                           extra_reads=["E8", stk])
            attend(k, [comp], 65, norm_consume(k, lambda g, h=h: (A[:, 4 * g:4 * g + 4, 256 + h * 64:256 + (h + 1) * 64],
                                                                   [("A", 4 * g + q) for q in range(4)])), f"moba{h}")
    build_AT(k, A, 4)


SH_NEEDED = ([(0, h) for h in range(8, 16)] + [(0, 16)] + [(1, h) for h in range(8, 16)] + [(2, h) for h in range(0, 4)]
             + [(3, h) for h in range(4, 8)] + [(4, h) for h in range(8, 12)])


def bias_setup(k):
    P, nc = k.P, k.nc
    k.ext = nc.dram_tensor("ext_scratch", [5, 32, LEXT], BF16, kind="Internal").ap()
    k.sh = nc.dram_tensor("sh_scratch", [len(SH_NEEDED), 128 * (LS + 1)], BF16, kind="Internal").ap()
    k.shc = nc.dram_tensor("shc_scratch", [8, 127 * (LC + 16)], BF16, kind="Internal").ap()
    k.sh_idx = {vh: i for i, vh in enumerate(SH_NEEDED)}
    k.c31 = P.sbuf("c31", [128, 16], F32)
    P.dma("sp", k.c31[:], k.din["t5_bias"][31, :].partition_broadcast(128), writes=["c31"])
    with P.phase():
        tab = P.sbuf("tab", [64, 128], F32)
        oh = P.sbuf("oh", [64, LEXT], F32)
        extsb = P.sbuf("extsb", [32, LEXT], BF16)
        P.memset("dve", tab[:], 0.0, writes=["tab"])
        P.memset("dve", oh[:], 0.0, writes=["oh"])
        P.memset("dve", tab[32:33, 0:17], 1.0, writes=["tab"])
        P.dma("sp", tab[0:32, 0:16], k.din["t5_bias"], writes=["tab"])
        bank = k.pb[7]
        for v in range(5):
            P.dma("sp", oh[0:33, :], k.din["k_oh"][v], writes=["oh"])
            for ch in range(8):
                P.mm(bank[:, :], tab[:, :], oh[:, ch * 512:(ch + 1) * 512], True, reads=["tab", "oh"], writes=["bank7"])
                P.copy("dve", extsb[:, ch * 512:(ch + 1) * 512], bank[0:32, :], reads=["bank7"], writes=["extsb"])
            P.dma("sp", k.ext[v], extsb[:], reads=["extsb"], writes=[("ext", v)])
        for (v, h), idx in k.sh_idx.items():
            dst = bass.AP(tensor=k.sh.tensor, offset=idx * 128 * (LS + 1), ap=[[LS + 1, 128], [1, LS]])
            src = bass.AP(tensor=k.ext.tensor, offset=(v * 32 + h) * LEXT + (OFF - 127), ap=[[0, 128], [1, LS]])
            P.dma("sp", dst, src, reads=[("ext", v)], writes=[("sh", idx)])
        for h in range(8, 16):
            dst = bass.AP(tensor=k.shc.tensor, offset=(h - 8) * 127 * (LC + 16), ap=[[LC + 16, 127], [1, LC]])
            src = bass.AP(tensor=k.ext.tensor, offset=h * LEXT, ap=[[0, 127], [1, LC]])
            P.dma("sp", dst, src, reads=[("ext", 0)], writes=[("shc", h)])


def load_tile(k, dst, v, h, delta, key):
    idx = k.sh_idx[(v, h)]
    src = bass.AP(tensor=k.sh.tensor, offset=idx * 128 * (LS + 1) + 127 + delta * 128, ap=[[LS, 128], [1, 128]])
    k.P.dma("sp", dst, src, reads=[("sh", idx)], writes=[key])


def attend(k, comps, vw, consume, tag):
    P, nc = k.P, k.nc
    st = k.att_state
    for g in range(4):
        ob = st["o"] % 2
        st["o"] += 1
        obank = k.pb[4 + ob]
        okey = f"bank{4 + ob}"
        ov = obank[:, 0:4 * vw].rearrange("p (q w) -> p q w", q=4)
        first = True
        for comp in comps:
            ksz = comp["ksz"]
            for kt in range(comp["nkt"]):
                subs = [(qi, comp["bias"](4 * g + qi - kt)) for qi in range(4)]
                subs = [(qi, b) for qi, b in subs if not (isinstance(b, str) and b == "skip")]
                if not subs:
                    continue
                sbi = st["s"] % 3
                st["s"] += 1
                sb = k.pb[sbi]
                skey = f"bank{sbi}"
                P.mm(sb[0:ksz, :], comp["KT"](kt), comp["QT"](g), True, False, reads=comp["reads"], writes=[skey])
                if comp.get("extra") is not None:
                    P.mm(sb[0:ksz, :], comp["extra"][0](kt), comp["extra"][1](g), False, False,
                         reads=comp["extra_reads"], writes=[skey])
                for qi, b in subs:
                    if b is not None:
                        P.mm(sb[0:ksz, qi * 128:(qi + 1) * 128], k.idb[0:ksz, 0:ksz], b[0], False, False,
                             reads=["idb", b[1]], writes=[skey])
                pi = st["p"] % 3
                st["p"] += 1
                pbuf = st["pbuf"][pi]
                cb = comp.get("cbias")
                if cb is None:
                    P.act(pbuf[0:ksz, :], sb[0:ksz, :], AF.Exp, reads=[skey], writes=[("pbuf", pi)])
                else:
                    P.act(pbuf[0:ksz, :], sb[0:ksz, :], AF.Exp, bias=cb[0:ksz, :], reads=[skey, "c31"],
                          writes=[("pbuf", pi)])
                for qi, b in subs:
                    P.mm(ov[:, qi, :], pbuf[0:ksz, qi * 128:(qi + 1) * 128], comp["V"](kt), first, False,
                         reads=[("pbuf", pi)] + comp["vreads"], writes=[okey])
                    first = False
        consume(g, ov, okey)


def head_rms(k, dst, src, n, gain_bc, reads, writes, tag, d=64, split=None, npart=128):
    P = k.P
    sq = k.rms_sq[0:npart, 0:n * d].rearrange("p (n d) -> p n d", d=d)
    ss = k.rms_ss[0:npart, :]
    mh = k.mhalf[0:npart, :]
    P.act(sq, src, AF.Square, reads=reads, writes=["rms_sq"])
    P.reduce("dve", ss[:, 0:n], sq, ALU.add, reads=["rms_sq"], writes=["rms_ss"])
    P.ts("dve", ss[:, 0:n], ss[:, 0:n], 1.0 / d, EPS, ALU.mult, ALU.add, reads=["rms_ss"], writes=["rms_ss"])
    P.tt("pool", ss[:, 0:n], ss[:, 0:n], mh.to_broadcast([npart, n]), ALU.pow, reads=["rms_ss", "mhalf"],
         writes=["rms_ss"])
    P.tt("dve", sq, src, ss[:, 0:n].unsqueeze(2).to_broadcast([npart, n, d]), ALU.mult,
         reads=reads + ["rms_ss"], writes=["rms_sq"])
    if split is None:
        P.tt("dve", dst, sq, gain_bc.unsqueeze(1).to_broadcast([npart, n, d]), ALU.mult,
             reads=["rms_sq", tag], writes=writes)
    else:
        hh, s_ = split
        P.tt("dve", dst, sq.rearrange("p (h s) d -> p h s d", h=hh),
             gain_bc.unsqueeze(1).unsqueeze(1).to_broadcast([128, hh, s_, d]), ALU.mult,
             reads=["rms_sq", tag], writes=writes)


_NC_CACHE = {}


def kernel(**inputs):
    from concourse.bass_utils import run_bass_kernel_spmd
    if "nc" not in _NC_CACHE:
        _NC_CACHE["nc"] = build(12)
    nc = _NC_CACHE["nc"]
    consts = host_consts()
    shared = {}
    for name in PARAM_SHAPES:
        if name in nc.used_inputs:
            shared[name] = np.ascontiguousarray(np.asarray(inputs[name], dtype=np.float32))
    for name, v in consts.items():
        if name in nc.used_inputs:
            shared[name] = v
    x = np.asarray(inputs["x"], dtype=np.float32)
    c = np.asarray(inputs["c"], dtype=np.float32)
    n = x.shape[0]
    in_maps = []
    for b in range(n):
        m = dict(shared)
        m["x"] = np.ascontiguousarray(x[b])
        m["c"] = np.ascontiguousarray(c[b])
        in_maps.append(m)
    res = run_bass_kernel_spmd(nc, in_maps, core_ids=list(range(n)))
    out = np.stack([np.asarray(r["out"], dtype=np.float32) for r in res.results], axis=0)
    return out
```
